# Optimizing a Trainium2 kernel written in Bass

```python
import jax, jax.numpy as jnp
from jax import lax
import numpy as np

D_MODEL = 2048
BATCH = 1
SEQ = 16384
DEPTH = 1
DEC_BATCH = 2
DEC_SEQ = 8192
PAST_LEN = 128

GRID_W = 64
N_HEADS = 8
N_KV_HEADS = 2
HEAD_DIM = 128
ATTN_WIDTH = N_HEADS * HEAD_DIM
KV_WIDTH = N_KV_HEADS * HEAD_DIM
ROPE_THETA = 10000.0
Q_BLOCK = 128
LRU_WIDTH = D_MODEL - ATTN_WIDTH
LRU_BLOCKS = 16
LRU_BLOCK_W = LRU_WIDTH // LRU_BLOCKS
CONV_W = 4
CONV_PAD_LEFT = 2
LRU_C = 8.0
MIX_WIDTH = ATTN_WIDTH + LRU_WIDTH
IN_WIDTH = ATTN_WIDTH + 2 * KV_WIDTH + 2 * LRU_WIDTH
N_KEYS = 128
N_EXPERTS = N_KEYS * N_KEYS
PEER_HEADS = 8
PEER_KEY_DIM = 128
PEER_HALF = PEER_KEY_DIM // 2
PEER_TOPK = 16
TOKEN_BLOCK = 128
EPS = 1e-6

kernel_name = 'hymba_hawk_axialgqa_peer_encoder'


def rmsnorm(x, g):
    xf = x.astype(jnp.float32)
    y = xf * lax.rsqrt(jnp.mean(xf * xf, axis=-1, keepdims=True) + EPS)
    return (y * g.astype(jnp.float32)).astype(x.dtype)


def axial_angles(seq):
    n_rows = seq // GRID_W
    row = jnp.repeat(jnp.arange(n_rows, dtype=jnp.float32), GRID_W)
    col = jnp.tile(jnp.arange(GRID_W, dtype=jnp.float32), n_rows)
    half = HEAD_DIM // 2
    inv = ROPE_THETA ** (-jnp.arange(0, half, 2, dtype=jnp.float32) / half)
    return row[:, None] * inv, col[:, None] * inv


def rope_1d(x, ang):
    m = ang.shape[-1]
    c = jnp.cos(ang)[:, None, :]
    s = jnp.sin(ang)[:, None, :]
    x1, x2 = x[..., :m], x[..., m:]
    return jnp.concatenate([x1 * c - x2 * s, x2 * c + x1 * s], axis=-1)


def axial_rope(x, ang_row, ang_col):
    half = HEAD_DIM // 2
    xf = x.astype(jnp.float32)
    y = jnp.concatenate([rope_1d(xf[..., :half], ang_row), rope_1d(xf[..., half:], ang_col)], axis=-1)
    return y.astype(x.dtype)


def attention(q, k, v):
    b, s = q.shape[0], q.shape[1]
    n_blk = s // Q_BLOCK
    grp = N_HEADS // N_KV_HEADS
    qb = q.reshape(b, n_blk, Q_BLOCK, N_KV_HEADS, grp, HEAD_DIM).transpose(1, 0, 2, 3, 4, 5)
    scale = HEAD_DIM ** -0.5

    def one_block(qi):
        sc = jnp.einsum('bqkgd,bskd->bkgqs', qi, k, preferred_element_type=jnp.float32) * scale
        p = jax.nn.softmax(sc, axis=-1).astype(v.dtype)
        return jnp.einsum('bkgqs,bskd->bqkgd', p, v)

    o = lax.map(one_block, qb)
    return o.transpose(1, 0, 2, 3, 4, 5).reshape(b, s, ATTN_WIDTH)


def centred_dwconv(x, w, bias):
    s = x.shape[1]
    xp = jnp.pad(x, ((0, 0), (CONV_PAD_LEFT, CONV_W - 1 - CONV_PAD_LEFT), (0, 0)))
    out = bias
    for tap in range(CONV_W):
        out = out + xp[:, tap:tap + s] * w[tap]
    return out


def _lru_combine(e1, e2):
    a1, b1 = e1
    a2, b2 = e2
    return a1 * a2, a2 * b1 + b2


def rglru_bidir(x, w_gate_a, b_gate_a, w_gate_x, b_gate_x, lru_lambda):
    b, s, _ = x.shape
    xb = x.reshape(b, s, LRU_BLOCKS, LRU_BLOCK_W)
    xf = x.astype(jnp.float32)

    def direction(d, reverse):
        r = jax.nn.sigmoid((jnp.einsum('bsnk,nkj->bsnj', xb, w_gate_a[d]).reshape(b, s, LRU_WIDTH)
                            + b_gate_a[d]).astype(jnp.float32))
        i = jax.nn.sigmoid((jnp.einsum('bsnk,nkj->bsnj', xb, w_gate_x[d]).reshape(b, s, LRU_WIDTH)
                            + b_gate_x[d]).astype(jnp.float32))
        log_a = -LRU_C * r * jax.nn.softplus(-lru_lambda[d].astype(jnp.float32))
        a = jnp.exp(log_a)
        u = jnp.sqrt(-jnp.expm1(2.0 * log_a)) * (i * xf)
        _, h = lax.associative_scan(_lru_combine, (a, u), reverse=reverse, axis=1)
        return h

    return (direction(0, False) + direction(1, True)).astype(x.dtype)


def peer(x, w_query, sub_keys, expert_down, expert_up):
    b, s, d = x.shape
    xt = x.reshape((b * s) // TOKEN_BLOCK, TOKEN_BLOCK, d)

    def one_block(xi):
        q = (xi @ w_query).reshape(TOKEN_BLOCK, PEER_HEADS, 2, PEER_HALF)
        sc = jnp.einsum('thcd,hckd->thck', q, sub_keys, preferred_element_type=jnp.float32)
        top_v, top_i = lax.top_k(sc, PEER_TOPK)
        cand = (top_v[:, :, 0, :, None] + top_v[:, :, 1, None, :]).reshape(TOKEN_BLOCK, PEER_HEADS, PEER_TOPK * PEER_TOPK)
        cand_id = (top_i[:, :, 0, :, None] * N_KEYS + top_i[:, :, 1, None, :]).reshape(TOKEN_BLOCK, PEER_HEADS, PEER_TOPK * PEER_TOPK)
        best, pos = lax.top_k(cand, PEER_TOPK)
        eid = jnp.take_along_axis(cand_id, pos, axis=-1)
        g = jax.nn.softmax(best, axis=-1)
        u = expert_down[eid]
        hid = jax.nn.gelu(jnp.einsum('thkd,td->thk', u, xi, preferred_element_type=jnp.float32), approximate=False)
        wgt = (g * hid).astype(x.dtype)
        return jnp.einsum('thk,thkd->td', wgt, expert_up[eid])

    return lax.map(one_block, xt).reshape(b, s, d)


def encoder_layer(x, ang_row, ang_col, g_mix, w_in, g_q, g_k, conv_w, conv_b,
                  w_gate_a, b_gate_a, w_gate_x, b_gate_x, lru_lambda,
                  g_attn_out, g_lru_out, w_out, g_ffn, w_query, sub_keys, expert_down, expert_up):
    b, s, _ = x.shape
    h = rmsnorm(x, g_mix)
    z = h @ w_in
    cuts = [ATTN_WIDTH, ATTN_WIDTH + KV_WIDTH, ATTN_WIDTH + 2 * KV_WIDTH, ATTN_WIDTH + 2 * KV_WIDTH + LRU_WIDTH]
    q, k, v, xr, yr = jnp.split(z, cuts, axis=-1)
    q = q.reshape(b, s, N_HEADS, HEAD_DIM)
    k = k.reshape(b, s, N_KV_HEADS, HEAD_DIM)
    v = v.reshape(b, s, N_KV_HEADS, HEAD_DIM)
    q = axial_rope(rmsnorm(q, g_q), ang_row, ang_col)
    k = axial_rope(rmsnorm(k, g_k), ang_row, ang_col)
    attn_out = attention(q, k, v)
    xr = centred_dwconv(xr, conv_w, conv_b)
    lru_out = rglru_bidir(xr, w_gate_a, b_gate_a, w_gate_x, b_gate_x, lru_lambda) * jax.nn.gelu(yr, approximate=False)
    mix = jnp.concatenate([rmsnorm(attn_out, g_attn_out), rmsnorm(lru_out, g_lru_out)], axis=-1)
    x = x + mix @ w_out
    x = x + peer(rmsnorm(x, g_ffn), w_query, sub_keys, expert_down, expert_up)
    return x


def trunk(x, g_mix, w_in, g_q, g_k, conv_w, conv_b, w_gate_a, b_gate_a, w_gate_x, b_gate_x, lru_lambda,
          g_attn_out, g_lru_out, w_out, g_ffn, w_query, sub_keys, expert_down, expert_up):
    ang_row, ang_col = axial_angles(x.shape[1])
    for l in range(DEPTH):
        x = encoder_layer(x, ang_row, ang_col, g_mix[l], w_in[l], g_q[l], g_k[l], conv_w[l], conv_b[l],
                          w_gate_a[l], b_gate_a[l], w_gate_x[l], b_gate_x[l], lru_lambda[l],
                          g_attn_out[l], g_lru_out[l], w_out[l], g_ffn[l], w_query[l], sub_keys[l],
                          expert_down[l], expert_up[l])
    return x


def setup_inputs(seed: int = 0) -> dict:
    key = jax.random.key(seed)
    ks = jax.random.split(key, 24)
    f32 = jnp.float32

    def nrm(k, shape, scale):
        return jax.random.normal(k, shape, f32) * scale

    def gain(k, shape):
        return 1.0 + 0.02 * jax.random.normal(k, shape, f32)

    a0 = jax.random.uniform(ks[13], (DEPTH, 2, LRU_WIDTH), f32, 0.9, 0.999)
    p = a0 ** (1.0 / LRU_C)
    lru_lambda = jnp.log(p) - jnp.log1p(-p)
    return {
        'x_prompt': jax.random.normal(ks[0], (BATCH, SEQ, D_MODEL), f32),
        'x_sample': jax.random.normal(ks[1], (DEC_BATCH, DEC_SEQ, D_MODEL), f32),
        'g_mix': gain(ks[2], (DEPTH, D_MODEL)),
        'w_in': nrm(ks[3], (DEPTH, D_MODEL, IN_WIDTH), D_MODEL ** -0.5),
        'g_q': gain(ks[4], (DEPTH, HEAD_DIM)),
        'g_k': gain(ks[5], (DEPTH, HEAD_DIM)),
        'conv_w': nrm(ks[6], (DEPTH, CONV_W, LRU_WIDTH), CONV_W ** -0.5),
        'conv_b': nrm(ks[7], (DEPTH, LRU_WIDTH), 0.01),
        'w_gate_a': nrm(ks[8], (DEPTH, 2, LRU_BLOCKS, LRU_BLOCK_W, LRU_BLOCK_W), LRU_BLOCK_W ** -0.5),
        'b_gate_a': nrm(ks[9], (DEPTH, 2, LRU_WIDTH), 0.01),
        'w_gate_x': nrm(ks[10], (DEPTH, 2, LRU_BLOCKS, LRU_BLOCK_W, LRU_BLOCK_W), LRU_BLOCK_W ** -0.5),
        'b_gate_x': nrm(ks[11], (DEPTH, 2, LRU_WIDTH), 0.01),
        'lru_lambda': lru_lambda,
        'g_attn_out': gain(ks[14], (DEPTH, ATTN_WIDTH)),
        'g_lru_out': gain(ks[15], (DEPTH, LRU_WIDTH)),
        'w_out': nrm(ks[16], (DEPTH, MIX_WIDTH, D_MODEL), MIX_WIDTH ** -0.5),
        'g_ffn': gain(ks[17], (DEPTH, D_MODEL)),
        'w_query': nrm(ks[18], (DEPTH, D_MODEL, PEER_HEADS * PEER_KEY_DIM), D_MODEL ** -0.5),
        'sub_keys': nrm(ks[19], (DEPTH, PEER_HEADS, 2, N_KEYS, PEER_HALF), PEER_HALF ** -0.5),
        'expert_down': nrm(ks[20], (DEPTH, N_EXPERTS, D_MODEL), D_MODEL ** -0.5),
        'expert_up': nrm(ks[21], (DEPTH, N_EXPERTS, D_MODEL), (PEER_HEADS * PEER_TOPK) ** -0.5),
    }


def reference(x_prompt, x_sample, g_mix, w_in, g_q, g_k, conv_w, conv_b, w_gate_a, b_gate_a,
              w_gate_x, b_gate_x, lru_lambda, g_attn_out, g_lru_out, w_out, g_ffn, w_query,
              sub_keys, expert_down, expert_up):
    y_prompt = trunk(x_prompt, g_mix, w_in, g_q, g_k, conv_w, conv_b, w_gate_a, b_gate_a, w_gate_x, b_gate_x,
                     lru_lambda, g_attn_out, g_lru_out, w_out, g_ffn, w_query, sub_keys, expert_down, expert_up)
    y_sample = trunk(x_sample, g_mix, w_in, g_q, g_k, conv_w, conv_b, w_gate_a, b_gate_a, w_gate_x, b_gate_x,
                     lru_lambda, g_attn_out, g_lru_out, w_out, g_ffn, w_query, sub_keys, expert_down, expert_up)
    return (y_prompt, y_sample)
```

```python
import numpy as np
from contextlib import ExitStack
import concourse.bass as bass
import concourse.mybir as mybir
from concourse.bass_utils import run_bass_kernel_spmd

F32 = mybir.dt.float32
BF16 = mybir.dt.bfloat16
U32 = mybir.dt.uint32
AF = mybir.ActivationFunctionType
ALU = mybir.AluOpType

D = 2048
DH = 128
NQH = 8
NKV = 2
INW = 3584
LRUW = 1024
NEXP = 16384
EPS = 1e-6
PAD = 512
BLK = 512
ARENA_F = 53000

GROUP = 8000
NSEM_PER_ENG = 20
NDMA_SEM = 48
NSW_SEM = 12

VG_MIX = 0
VG_Q = 16
VG_K = 17
VCONVW = 18
VCONVB = 50
VBA = 58
VBX = 74
VLAM = 90
VGA = 106
VGL = 114
NVEC = 122


class Res:
    __slots__ = ("w", "rs", "name")

    def __init__(self, name=""):
        self.w = None
        self.rs = {}
        self.name = name


class _Rec:
    def __init__(self):
        self.call = None

    def __getattr__(self, name):
        def f(*a, **k):
            assert self.call is None
            self.call = (name, a, k)
            return self
        return f


def _record(fn):
    r = _Rec()
    fn(r)
    assert r.call is not None
    return r.call


class Q:
    def __init__(self, fw, name, skip_self):
        self.fw = fw
        self.name = name
        self.n = 0
        self.prog = []
        self.waited = {}
        self.skip_self = skip_self
        self.pending = []

    def ev_for(self, idx):
        g = (idx - 1) // GROUP
        return ((self.name, g), idx - g * GROUP)

    def _collect(self, reads, writes):
        deps = []
        for r in reads:
            if r.w is not None:
                deps.append(r.w)
        for w in writes:
            if w.w is not None:
                deps.append(w.w)
            deps.extend(w.rs.values())
        if self.pending:
            deps.extend(self.pending)
            self.pending = []
        return deps

    def _filter(self, deps):
        need = {}
        for (key, val) in deps:
            if self.skip_self and key[0] == self.name:
                continue
            if self.waited.get(key, 0) >= val:
                continue
            if need.get(key, 0) < val:
                need[key] = val
        for key, val in need.items():
            self.waited[key] = val
            if key[0] in ("pe", "act", "dve", "pool", "sp"):
                for g in range(key[1]):
                    self.waited[(key[0], g)] = GROUP
        return list(need.items())

    def op(self, fn, reads=(), writes=(), ww=()):
        waits = self._filter(self._collect(reads, writes))
        self.n += 1
        ev = self.ev_for(self.n)
        self.prog.append((waits, _record(fn), ev, 1))
        for r in reads:
            r.rs[self.name] = ev
        for w in writes:
            w.w = ev
            w.rs = {}
        for w in ww:
            w.w = ev
        return ev

    def dma(self, fn, reads=(), writes=()):
        fw = self.fw
        deps = self._collect(reads, writes)
        if self.name == "pool":
            j = NDMA_SEM - NSW_SEM + fw.sw_next % NSW_SEM
            fw.sw_next += 1
        else:
            j = fw.dma_next % (NDMA_SEM - NSW_SEM)
            fw.dma_next += 1
        key = ("dma", j)
        if fw.dma_cnt[j] > 0:
            deps.append((key, fw.dma_cnt[j]))
        waits = self._filter(deps)
        fw.dma_cnt[j] += 16
        ev = (key, fw.dma_cnt[j])
        self.prog.append((waits, _record(fn), ev, 16))
        for r in reads:
            r.rs[("dmar", j)] = ev
        for w in writes:
            w.w = ev
            w.rs = {}
        return ev


def _swdma(self, fn, slot, reads=(), writes=()):
    fw = self.fw
    waits = self._filter(self._collect(reads, writes))
    fw.sw_gen[slot] = fw.sw_gen.get(slot, 0) + 1
    ev = (("swdma", slot, fw.sw_gen[slot]), 16)
    self.prog.append((waits, ("__swdma__", slot, _record(fn)), ev, 16))
    for r in reads:
        r.rs[("swr", slot)] = ev
    for w in writes:
        w.w = ev
        w.rs = {}
    return ev


Q.swdma = _swdma


class FW:
    def __init__(self, nc):
        self.nc = nc
        self.pe = Q(self, "pe", True)
        self.act = Q(self, "act", False)
        self.dve = Q(self, "dve", False)
        self.pool = Q(self, "pool", False)
        self.sp = Q(self, "sp", False)
        self.queues = [self.pe, self.act, self.dve, self.pool, self.sp]
        self.dma_next = 0
        self.sw_next = 0
        self.dma_cnt = [0] * NDMA_SEM
        self.sw_gen = {}

    def all_events(self):
        evs = []
        for q in self.queues:
            if q.n > 0:
                evs.append(q.ev_for(q.n))
        for j in range(NDMA_SEM):
            if self.dma_cnt[j] > 0:
                evs.append((("dma", j), self.dma_cnt[j]))
        for slot, gen in self.sw_gen.items():
            evs.append((("swdma", slot, gen), 16))
        return evs

    def barrier(self):
        evs = self.all_events()
        for q in self.queues:
            q.pending.extend(evs)

    def emit(self, stack):
        nc = self.nc
        sems = {}
        for q in self.queues:
            ng = max(1, (q.n + GROUP - 1) // GROUP)
            assert ng <= NSEM_PER_ENG, (q.name, q.n)
            for g in range(ng):
                sems[(q.name, g)] = stack.enter_context(nc.semaphore(f"s_{q.name}{g}"))
        for j in range(NDMA_SEM):
            sems[("dma", j)] = stack.enter_context(nc.semaphore(f"s_dma{j}"))
        for slot in self.sw_gen:
            sems[("swdma", slot)] = stack.enter_context(nc.semaphore(f"s_sw{slot}"))

        def semof(key):
            return sems[key[:2]] if key[0] == "swdma" else sems[key]
        block = stack.enter_context(nc.Block())
        handles = {"pe": block.tensor, "act": block.scalar, "dve": block.vector,
                   "pool": block.gpsimd, "sp": block.sync}
        final = self.all_events()

        def make(q):
            def body(eng):
                for (waits, call, ev, inc) in q.prog:
                    for (key, val) in waits:
                        eng.wait_ge(semof(key), val)
                    if call[0] == "__swdma__":
                        if ev[0][2] > 1:
                            eng.wait_ge(semof(ev[0]), 16)
                        eng.sem_clear(semof(ev[0]))
                        call = call[2]
                    ins = getattr(eng, call[0])(*call[1], **call[2])
                    ins.then_inc(semof(ev[0]), inc)
                if q.name == "sp":
                    for (key, val) in final:
                        eng.wait_ge(semof(key), val)
            return body

        for q in self.queues:
            handles[q.name](make(q))


class Arena:
    def __init__(self, ap_f32, nfloats, fw=None):
        self.base = ap_f32
        self.n = nfloats
        self.off = 0
        self.marks = []
        self.fw = fw

    def push(self):
        self.marks.append(self.off)

    def pop(self):
        self.off = self.marks.pop()
        if self.fw is not None:
            self.fw.barrier()

    def f32(self, n, name=""):
        n8 = (n + 7) // 8 * 8
        assert self.off + n8 <= self.n, f"SBUF arena overflow at {name}: {self.off}+{n8}>{self.n}"
        ap = self.base[:, self.off:self.off + n]
        self.off += n8
        return ap, Res(name)

    def bf16(self, n, name=""):
        nf = (n + 1) // 2
        ap, r = self.f32(nf, name)
        return ap.bitcast(BF16)[:, 0:n], r

    def u32(self, n, name=""):
        ap, r = self.f32(n, name)
        return ap.bitcast(U32), r


class Cfg:
    def __init__(self, seq_lens=(16384, 8192, 8192), ncores=8):
        self.seq_lens = list(seq_lens)
        self.ncores = ncores
        self.lo = [L // ncores for L in self.seq_lens]
        self.lv = [L + PAD for L in self.seq_lens]
        self.voff = [0]
        for v in self.lv:
            self.voff.append(self.voff[-1] + v)
        self.nv = self.voff[-1]
        self.ooff = [0]
        for o in self.lo:
            self.ooff.append(self.ooff[-1] + o)
        self.nown = self.ooff[-1]
        for o in self.lo:
            assert o % BLK == 0, o


def build_program(cfg, phases=("W", "P0", "P1", "P2", "P3a", "P3b"), debug_outs=()):
    nc = bass.Bass("TRN2", target_bir_lowering=False)
    NV, NOWN = cfg.nv, cfg.nown

    def din(name, shape, dt=F32):
        return nc.dram_tensor(name, list(shape), dt, kind="ExternalInput").ap()

    def dscr(name, shape, dt):
        kind = "ExternalOutput" if name in debug_outs else "Internal"
        return nc.dram_tensor(name, list(shape), dt, kind=kind).ap()

    xv = din("xv", [NV, D])
    cs = din("cs", [2, 128, NV])
    kmask = din("kmask", [128, NV // 128])
    lmask = din("lmask", [2, NV])
    vecs_d = din("vecs", [128, NVEC])
    w_in = din("w_in", [D, INW])
    w_out = din("w_out", [D, D])
    w_query = din("w_query", [D, 1024])
    wg_d = din("wg", [32, 128, 128])
    rk_d = din("rk", [8, 128, 256])
    gffn_d = din("gffn", [1, D])
    ident_d = din("ident", [128, 128])
    prot_d = din("prot", [128, 128])
    edown = din("edown", [NEXP, D])
    eup = din("eup", [NEXP, D])
    y = nc.dram_tensor("y", [NOWN, D], F32, kind="ExternalOutput").ap()

    hT = dscr("hT", [16, 128, NV], BF16)
    winb = dscr("winb", [16, 128, INW], BF16)
    mixT = dscr("mixT", [16, 128, NOWN], BF16)
    x2d = dscr("x2d", [NOWN, D], F32)
    ebf = dscr("ebf", [NEXP, 2 * D], BF16)

    with ExitStack() as stack:
        arena_t = stack.enter_context(nc.sbuf_tensor("arena", [128, ARENA_F], F32))
        ps = []
        for i in range(8):
            p = stack.enter_context(nc.psum_tensor(f"ps{i}", [128, 512], F32))
            ps.append((p[:], Res(f"ps{i}")))
        fw = FW(nc)
        ar = Arena(arena_t[:], ARENA_F, fw)
        pe, act, dve, pool, sp = fw.pe, fw.act, fw.dve, fw.pool, fw.sp

        vecs = ar.f32(NVEC, "vecs")
        idf = ar.f32(128, "idf")
        idb = ar.bf16(128, "idb")
        onesb = ar.bf16(128, "onesb")
        onesf = ar.f32(128, "onesf")
        cst = ar.f32(8, "cst")
        rotq = ar.bf16(128, "rotq")
        rotk = ar.bf16(128, "rotk")
        cdec = ar.f32(16, "cdec")
        hvec = ar.f32(48, "hvec")
        sp.dma(lambda e: e.dma_start(out=vecs[0], in_=vecs_d), writes=[vecs[1]])
        sp.dma(lambda e: e.dma_start(out=idf[0], in_=ident_d), writes=[idf[1]])
        dve.op(lambda e: e.tensor_copy(out=idb[0], in_=idf[0]), reads=[idf[1]], writes=[idb[1]])
        dve.op(lambda e: e.memset(onesb[0], 1.0), writes=[onesb[1]])
        dve.op(lambda e: e.memset(onesf[0], 1.0), writes=[onesf[1]])
        dve.op(lambda e: e.memset(cst[0][:, 0:1], EPS), writes=[cst[1]])
        dve.op(lambda e: e.memset(cst[0][:, 1:2], 1.0), writes=[cst[1]])
        dve.op(lambda e: e.memset(cst[0][:, 2:3], 0.25), writes=[cst[1]])
        ar.push()
        tmpc = ar.f32(128, "tmpc")
        sp.dma(lambda e: e.dma_start(out=tmpc[0], in_=prot_d), writes=[tmpc[1]])
        dve.op(lambda e: e.tensor_scalar(out=rotq[0], in0=tmpc[0], scalar1=vecs[0][:, VG_Q:VG_Q + 1], scalar2=None, op0=ALU.mult),
               reads=[tmpc[1], vecs[1]], writes=[rotq[1]])
        dve.op(lambda e: e.tensor_scalar(out=rotk[0], in0=tmpc[0], scalar1=vecs[0][:, VG_K:VG_K + 1], scalar2=None, op0=ALU.mult),
               reads=[tmpc[1], vecs[1]], writes=[rotk[1]])
        t_e = ar.f32(16, "t_e"); t_z = ar.f32(16, "t_z"); t_z2 = ar.f32(16, "t_z2"); t_p = ar.f32(16, "t_p")
        lam = vecs[0][:, VLAM:VLAM + 16]
        act.op(lambda e: e.activation(out=t_e[0], in_=lam, func=AF.Exp, scale=-1.0), reads=[vecs[1]], writes=[t_e[1]])
        dve.op(lambda e: e.tensor_scalar(out=t_z[0], in0=t_e[0], scalar1=2.0, scalar2=None, op0=ALU.add), reads=[t_e[1]], writes=[t_z[1]])
        dve.op(lambda e: e.reciprocal(out=t_z[0], in_=t_z[0]), reads=[t_z[1]], writes=[t_z[1]])
        dve.op(lambda e: e.tensor_tensor(out=t_z[0], in0=t_z[0], in1=t_e[0], op=ALU.mult), reads=[t_z[1], t_e[1]], writes=[t_z[1]])
        dve.op(lambda e: e.tensor_tensor(out=t_z2[0], in0=t_z[0], in1=t_z[0], op=ALU.mult), reads=[t_z[1]], writes=[t_z2[1]])
        dve.op(lambda e: e.tensor_scalar(out=t_p[0], in0=t_z2[0], scalar1=1.0 / 13.0, scalar2=1.0 / 11.0, op0=ALU.mult, op1=ALU.add),
               reads=[t_z2[1]], writes=[t_p[1]])
        for cf in (1.0 / 9.0, 1.0 / 7.0, 1.0 / 5.0, 1.0 / 3.0, 1.0):
            dve.op(lambda e: e.tensor_tensor(out=t_p[0], in0=t_p[0], in1=t_z2[0], op=ALU.mult), reads=[t_p[1], t_z2[1]], writes=[t_p[1]])
            dve.op(lambda e, cf=cf: e.tensor_scalar(out=t_p[0], in0=t_p[0], scalar1=cf, scalar2=None, op0=ALU.add), reads=[t_p[1]], writes=[t_p[1]])
        dve.op(lambda e: e.tensor_tensor(out=t_p[0], in0=t_p[0], in1=t_z[0], op=ALU.mult), reads=[t_p[1], t_z[1]], writes=[t_p[1]])
        dve.op(lambda e: e.tensor_scalar(out=cdec[0], in0=t_p[0], scalar1=-16.0, scalar2=None, op0=ALU.mult), reads=[t_p[1]], writes=[cdec[1]])
        dve.op(lambda e: e.tensor_scalar(out=hvec[0][:, 0:32], in0=vecs[0][:, VBA:VBA + 32], scalar1=0.5, scalar2=None, op0=ALU.mult), reads=[vecs[1]], writes=[hvec[1]])
        dve.op(lambda e: e.tensor_scalar(out=hvec[0][:, 32:48], in0=cdec[0], scalar1=0.5, scalar2=None, op0=ALU.mult), reads=[cdec[1]], writes=[hvec[1]])
        ar.pop()

        if "W" in phases:
            ar.push()
            wst = [ar.f32(INW, f"wst{i}") for i in range(2)]
            wbf = [ar.bf16(INW, f"wbf{i}") for i in range(2)]
            for kc in range(16):
                a = wst[kc % 2]; b = wbf[kc % 2]
                sp.dma(lambda e, a=a, kc=kc: e.dma_start(out=a[0], in_=w_in[kc * 128:(kc + 1) * 128, :]), writes=[a[1]])
                q = dve if kc % 2 == 0 else pool
                q.op(lambda e, a=a, b=b, kc=kc: e.tensor_scalar(out=b[0], in0=a[0], scalar1=vecs[0][:, VG_MIX + kc:VG_MIX + kc + 1], scalar2=None, op0=ALU.mult),
                     reads=[a[1], vecs[1]], writes=[b[1]])
                sp.dma(lambda e, b=b, kc=kc: e.dma_start(out=winb[kc], in_=b[0]), reads=[b[1]])
            ar.pop()
            ebf_res = Res("ebf")
            for t, table in enumerate((edown, eup)):
                for ch in range(4):
                    rows = slice(ch * 4096, (ch + 1) * 4096)
                    pool.dma(lambda e, rows=rows, t=t, table=table: e.dma_start(out=ebf[rows, t * D:(t + 1) * D], in_=table[rows, :]))

        if "P0" in phases:
            ar.push()
            NXB = 3
            xb = [ar.f32(D, f"x{i}") for i in range(NXB)]
            xn = [ar.bf16(D, f"xn{i}") for i in range(2)]
            junk = ar.bf16(D, "junk")
            st = [ar.bf16(16 * BLK, f"hTs{i}") for i in range(2)]
            stat = [ar.f32(4, f"stat{i}") for i in range(4)]
            ntile = NV // 128

            def load(i):
                b = xb[i % NXB]
                sp.dma(lambda e, b=b, i=i: e.dma_start(out=b[0], in_=xv[i * 128:(i + 1) * 128, :]), writes=[b[1]])
            load(0)
            load(1)
            for i in range(ntile):
                if i + 2 < ntile:
                    load(i + 2)
                b = xb[i % NXB]; n = xn[i % 2]; s = stat[i % 4]; stg = st[(i // 4) % 2]
                act.op(lambda e, b=b, s=s: e.activation(out=junk[0], in_=b[0], func=AF.Square, accum_out=s[0][:, 0:1]),
                       reads=[b[1]], writes=[junk[1], s[1]])
                act.op(lambda e, s=s: e.activation(out=s[0][:, 1:2], in_=s[0][:, 0:1], func=AF.Sqrt, bias=cst[0][:, 0:1], scale=1.0 / D),
                       reads=[s[1], cst[1]], writes=[s[1]])
                dve.op(lambda e, s=s: e.reciprocal(out=s[0][:, 2:3], in_=s[0][:, 1:2]), reads=[s[1]], writes=[s[1]])
                dve.op(lambda e, b=b, n=n, s=s: e.tensor_scalar(out=n[0], in0=b[0], scalar1=s[0][:, 2:3], scalar2=None, op0=ALU.mult),
                       reads=[b[1], s[1]], writes=[n[1]])
                for half in range(2):
                    bank = ps[(i % 2) * 2 + half]
                    pv = bank[0].bitcast(BF16)
                    for j in range(8):
                        kc = half * 8 + j
                        pe.op(lambda e, pv=pv, n=n, j=j, kc=kc: e.transpose(out=pv[:, j * 128:(j + 1) * 128], in_=n[0][:, kc * 128:(kc + 1) * 128], identity=idb[0]),
                              reads=[n[1], idb[1]], writes=[bank[1]])
                    dst = stg[0].rearrange("p (k t) -> p k t", k=16)[:, half * 8:(half + 1) * 8, (i % 4) * 128:(i % 4 + 1) * 128]
                    src = pv.rearrange("p (k t) -> p k t", k=8)
                    if half == 0:
                        act.op(lambda e, dst=dst, src=src: e.copy(out=dst, in_=src), reads=[bank[1]], writes=[stg[1]])
                    else:
                        dve.op(lambda e, dst=dst, src=src: e.tensor_copy(out=dst, in_=src), reads=[bank[1]], writes=[stg[1]])
                if i % 4 == 3:
                    t0 = (i // 4) * BLK
                    sp.dma(lambda e, stg=stg, t0=t0: e.dma_start(out=hT[:, :, t0:t0 + BLK].rearrange("k p t -> p k t"),
                                                                  in_=stg[0].rearrange("p (k t) -> p k t", k=16)), reads=[stg[1]])
            ar.pop()
            fw.barrier()

        def load_w(dst, col0, ncol, q=None):
            (q or sp).dma(lambda e: e.dma_start(out=dst[0].rearrange("p (k c) -> p k c", k=16),
                                                in_=winb[:, :, col0:col0 + ncol].rearrange("k p c -> p k c")), writes=[dst[1]])

        def proj(bank, wt, ncol, c0, hb, n=BLK, cw=128):
            for kc in range(16):
                pe.op(lambda e, kc=kc: e.matmul(bank[0][0:cw, 0:n], lhsT=wt[0][:, kc * ncol + c0:kc * ncol + c0 + cw],
                                                rhs=hb[0][:, kc * BLK:kc * BLK + n], start=(kc == 0), stop=(kc == 15)),
                      reads=[wt[1], hb[1]], writes=[bank[1]])

        if "P1" in phases:
            for s in range(len(cfg.seq_lens)):
                LV, LO, V0, O0 = cfg.lv[s], cfg.lo[s], cfg.voff[s], cfg.ooff[s]
                NB = LV // BLK
                QB = min(BLK, LO)
                NQB = LO // QB
                for g in range(NKV):
                    ar.push()
                    KT = ar.bf16(LV, "KT")
                    Vt = ar.bf16(LV, "V")
                    QT = ar.bf16(4 * LO, "QT")
                    wq = ar.bf16(16 * 512, "wq")
                    wk = ar.bf16(16 * 128, "wk")
                    wv = ar.bf16(16 * 128, "wv")
                    km = ar.f32(LV // 128, "km")
                    hbs = [ar.bf16(16 * BLK, f"hb{i}") for i in range(2)]
                    cst_cs = [ar.f32(2 * BLK, f"cs{i}") for i in range(2)]
                    zb = [ar.bf16(BLK, f"zb{i}") for i in range(2)]
                    sq = [ar.bf16(BLK, f"sq{i}") for i in range(2)]
                    rstd = [ar.f32(BLK, f"rstd{i}") for i in range(2)]
                    t1 = [ar.f32(BLK, f"t1{i}") for i in range(2)]
                    t2 = [ar.f32(BLK, f"t2{i}") for i in range(2)]
                    vt = ar.bf16(BLK, "vt")
                    pt = [ar.bf16(BLK, f"pt{i}") for i in range(3)]
                    rs = [ar.f32(BLK, f"rs{i}") for i in range(2)]
                    ob = [ar.bf16(BLK, f"ob{i}") for i in range(2)]
                    pacc = [ar.f32(BLK, f"pacc{i}") for i in range(4)]
                    load_w(wq, g * 512, 512)
                    load_w(wk, 1024 + g * 128, 128)
                    load_w(wv, 1280 + g * 128, 128)
                    sp.dma(lambda e: e.dma_start(out=km[0], in_=kmask[:, V0 // 128:(V0 + LV) // 128]), writes=[km[1]])
                    cnt = [0]

                    def qknorm_rope(bank, gcol, rot, csb, dst):
                        i = cnt[0] % 2
                        cnt[0] += 1
                        z = bank[0]
                        act.op(lambda e: e.copy(out=zb[i][0], in_=z), reads=[bank[1]], writes=[zb[i][1]])
                        act.op(lambda e: e.activation(out=sq[i][0], in_=z, func=AF.Square), reads=[bank[1]], writes=[sq[i][1]])
                        pe.op(lambda e: e.matmul(ps[2][0], lhsT=onesb[0], rhs=sq[i][0], start=True, stop=True),
                              reads=[onesb[1], sq[i][1]], writes=[ps[2][1]])
                        pe.op(lambda e: e.matmul(ps[3][0], lhsT=rot[0], rhs=zb[i][0], start=True, stop=True),
                              reads=[rot[1], zb[i][1]], writes=[ps[3][1]])
                        act.op(lambda e: e.activation(out=rstd[i][0], in_=ps[2][0], func=AF.Sqrt, bias=cst[0][:, 0:1], scale=1.0 / DH),
                               reads=[ps[2][1], cst[1]], writes=[rstd[i][1]])
                        dve.op(lambda e: e.reciprocal(out=rstd[i][0], in_=rstd[i][0]), reads=[rstd[i][1]], writes=[rstd[i][1]])
                        dve.op(lambda e: e.scalar_tensor_tensor(out=t1[i][0], in0=z, scalar=gcol, in1=csb[0][:, 0:BLK], op0=ALU.mult, op1=ALU.mult),
                               reads=[bank[1], vecs[1], csb[1]], writes=[t1[i][1]])
                        dve.op(lambda e: e.tensor_tensor(out=t2[i][0], in0=ps[3][0], in1=csb[0][:, BLK:2 * BLK], op=ALU.mult),
                               reads=[ps[3][1], csb[1]], writes=[t2[i][1]])
                        pool.op(lambda e: e.tensor_tensor(out=t1[i][0], in0=t1[i][0], in1=t2[i][0], op=ALU.add),
                                reads=[t1[i][1], t2[i][1]], writes=[t1[i][1]])
                        pool.op(lambda e: e.tensor_tensor(out=dst, in0=t1[i][0], in1=rstd[i][0], op=ALU.mult),
                                reads=[t1[i][1], rstd[i][1]], writes=[dst_res[0]])

                    dst_res = [None]

                    def ldblk(b):
                        hb = hbs[b % 2]; c = cst_cs[b % 2]
                        t0 = V0 + b * BLK
                        sp.dma(lambda e: e.dma_start(out=hb[0].rearrange("p (k t) -> p k t", k=16),
                                                     in_=hT[:, :, t0:t0 + BLK].rearrange("k p t -> p k t")), writes=[hb[1]])
                        sp.dma(lambda e: e.dma_start(out=c[0].rearrange("p (a t) -> p a t", a=2),
                                                     in_=cs[:, :, t0:t0 + BLK].rearrange("a p t -> p a t")), writes=[c[1]])
                    ldblk(0)
                    zi = 0
                    for b in range(NB):
                        if b + 1 < NB:
                            ldblk(b + 1)
                        hb = hbs[b % 2]; c = cst_cs[b % 2]
                        bank = ps[zi % 2]; zi += 1
                        proj(bank, wk, 128, 0, hb)
                        dst_res[0] = KT[1]
                        qknorm_rope(bank, vecs[0][:, VG_K:VG_K + 1], rotk, c, KT[0][:, b * BLK:(b + 1) * BLK])
                        bank = ps[zi % 2]; zi += 1
                        proj(bank, wv, 128, 0, hb)
                        act.op(lambda e, bank=bank: e.copy(out=vt[0], in_=bank[0]), reads=[bank[1]], writes=[vt[1]])
                        pv = ps[4][0].bitcast(BF16)
                        for j in range(4):
                            pe.op(lambda e, j=j, pv=pv: e.transpose(out=pv[:, j * 128:(j + 1) * 128], in_=vt[0][:, j * 128:(j + 1) * 128], identity=idb[0]),
                                  reads=[vt[1], idb[1]], writes=[ps[4][1]])
                        dve.op(lambda e, b=b, pv=pv: e.tensor_copy(out=Vt[0][:, b * BLK:(b + 1) * BLK], in_=pv[:, 0:BLK]),
                               reads=[ps[4][1]], writes=[Vt[1]])
                        if b * BLK < LO:
                            for hh in range(4):
                                bank = ps[zi % 2]; zi += 1
                                proj(bank, wq, 512, hh * 128, hb, n=QB)
                                dst_res[0] = QT[1]
                                qknorm_rope(bank, vecs[0][:, VG_Q:VG_Q + 1], rotq, c, QT[0][:, hh * LO + b * BLK:hh * LO + b * BLK + QB])
                    it = 0
                    for hh in range(4):
                        for qb in range(NQB):
                            Ob = ps[(it % 2) * 2]; Sb = ps[(it % 2) * 2 + 1]
                            r = rs[it % 2]; o = ob[it % 2]
                            qsl = QT[0][:, hh * LO + qb * QB:hh * LO + (qb + 1) * QB]
                            nkc = LV // 128

                            def S_(kc):
                                Sk = ps[5 + kc % 3]
                                pe.op(lambda e: e.matmul(Sk[0][:, 0:QB], lhsT=KT[0][:, kc * 128:(kc + 1) * 128], rhs=qsl, start=True, stop=True),
                                      reads=[KT[1], QT[1]], writes=[Sk[1]])

                            def E_(kc):
                                Sk = ps[5 + kc % 3]; p = pt[kc % 3]
                                act.op(lambda e: e.activation(out=p[0][:, 0:QB], in_=Sk[0][:, 0:QB], func=AF.Exp, bias=km[0][:, kc:kc + 1], scale=DH ** -0.5),
                                       reads=[Sk[1], km[1]], writes=[p[1]])

                            def OV_(kc):
                                p = pt[kc % 3]
                                pe.op(lambda e: e.matmul(Ob[0][:, 0:QB], lhsT=Vt[0][:, kc * 128:(kc + 1) * 128], rhs=p[0][:, 0:QB], start=(kc == 0), stop=(kc == nkc - 1)),
                                      reads=[Vt[1], p[1]], writes=[Ob[1]])
                                pa = pacc[(it % 2) * 2 + kc % 2]
                                qa = dve if kc % 2 == 0 else pool
                                if kc < 2:
                                    qa.op(lambda e: e.tensor_copy(out=pa[0][:, 0:QB], in_=p[0][:, 0:QB]), reads=[p[1]], writes=[pa[1]])
                                else:
                                    qa.op(lambda e: e.tensor_tensor(out=pa[0][:, 0:QB], in0=pa[0][:, 0:QB], in1=p[0][:, 0:QB], op=ALU.add), reads=[p[1], pa[1]], writes=[pa[1]])

                            S_(0)
                            if nkc > 1:
                                S_(1)
                            for kc in range(nkc):
                                E_(kc)
                                if kc + 2 < nkc:
                                    S_(kc + 2)
                                OV_(kc)
                            pa0 = pacc[(it % 2) * 2]; pa1 = pacc[(it % 2) * 2 + 1]
                            if nkc > 1:
                                dve.op(lambda e, pa0=pa0, pa1=pa1: e.tensor_tensor(out=pa0[0][:, 0:QB], in0=pa0[0][:, 0:QB], in1=pa1[0][:, 0:QB], op=ALU.add), reads=[pa0[1], pa1[1]], writes=[pa0[1]])
                            pe.op(lambda e, pa0=pa0, Sb=Sb: e.matmul(Sb[0][:, 0:QB], lhsT=onesf[0], rhs=pa0[0][:, 0:QB], start=True, stop=True),
                                  reads=[onesf[1], pa0[1]], writes=[Sb[1]])
                            dve.op(lambda e, r=r, Sb=Sb: e.reciprocal(out=r[0][:, 0:QB], in_=Sb[0][:, 0:QB]), reads=[Sb[1]], writes=[r[1]])
                            dve.op(lambda e, r=r, o=o, Ob=Ob: e.tensor_tensor(out=o[0][:, 0:QB], in0=Ob[0][:, 0:QB], in1=r[0][:, 0:QB], op=ALU.mult),
                                   reads=[Ob[1], r[1]], writes=[o[1]])
                            head = g * 4 + hh
                            t0 = O0 + qb * QB
                            sp.dma(lambda e, o=o, head=head, t0=t0: e.dma_start(out=mixT[head, :, t0:t0 + QB], in_=o[0][:, 0:QB]), reads=[o[1]])
                            it += 1
                    ar.pop()
            fw.barrier()

        if "P2" in phases:
            for s in range(len(cfg.seq_lens)):
                LV, LO, V0, O0 = cfg.lv[s], cfg.lo[s], cfg.voff[s], cfg.ooff[s]
                NB = LV // BLK
                QB = min(BLK, LO)
                NOB = LO // QB
                assert QB == BLK or NOB == 1
                for c in range(8):
                    ar.push()
                    xc = ar.f32(LV, "xc")
                    gy = ar.f32(LO, "gy")
                    hown = [ar.f32(LO, f"hown{d}") for d in range(2)]
                    wxr = ar.bf16(16 * 128, "wxr")
                    wyr = ar.bf16(16 * 128, "wyr")
                    gst = ar.f32(128, "gst")
                    gm = [ar.bf16(128, f"gm{i}") for i in range(4)]
                    stt = [ar.f32(8, f"state{d}") for d in range(2)]
                    lo_t = [ar.bf16(BLK, f"lo{i}") for i in range(2)]
                    ar.push()
                    hbs = [ar.bf16(16 * BLK, f"hb{i}") for i in range(2)]
                    S = [ar.f32(BLK + 8, f"S{i}") for i in range(2)]
                    first = ar.f32(8, "first")
                    load_w(wxr, 1536 + c * 128, 128)
                    load_w(wyr, 2560 + c * 128, 128)
                    for gi in range(4):
                        sp.dma(lambda e, gi=gi: e.dma_start(out=gst[0], in_=wg_d[gi * 8 + c]), writes=[gst[1]])
                        dve.op(lambda e, gi=gi: e.tensor_copy(out=gm[gi][0], in_=gst[0]), reads=[gst[1]], writes=[gm[gi][1]])
                    cw = [vecs[0][:, VCONVW + tap * 8 + c:VCONVW + tap * 8 + c + 1] for tap in range(4)]
                    cb = vecs[0][:, VCONVB + c:VCONVB + c + 1]

                    def ldblk(j, b):
                        hb = hbs[j % 2]
                        t0 = V0 + b * BLK
                        sp.dma(lambda e: e.dma_start(out=hb[0].rearrange("p (k t) -> p k t", k=16),
                                                     in_=hT[:, :, t0:t0 + BLK].rearrange("k p t -> p k t")), writes=[hb[1]])

                    def conv(Sx, c0, n, dst):
                        dve.op(lambda e: e.tensor_scalar(out=dst, in0=Sx[0][:, c0 - 2:c0 - 2 + n], scalar1=cw[0], scalar2=cb, op0=ALU.mult, op1=ALU.add),
                               reads=[Sx[1], vecs[1]], writes=[xc[1]])
                        for tap in range(1, 4):
                            dve.op(lambda e, tap=tap: e.scalar_tensor_tensor(out=dst, in0=Sx[0][:, c0 - 2 + tap:c0 - 2 + tap + n], scalar=cw[tap], in1=dst, op0=ALU.mult, op1=ALU.add),
                                   reads=[Sx[1], vecs[1], xc[1]], writes=[xc[1]])

                    order = [NB - 1] + list(range(NB))
                    ldblk(0, order[0])
                    zi = 0
                    for j, b in enumerate(order):
                        if j + 1 < len(order):
                            ldblk(j + 1, order[j + 1])
                        hb = hbs[j % 2]; Sx = S[j % 2]; Sp = S[(j + 1) % 2]
                        bank = ps[zi % 2]; zi += 1
                        proj(bank, wxr, 128, 0, hb)
                        act.op(lambda e, Sx=Sx, bank=bank: e.copy(out=Sx[0][:, 3:3 + BLK], in_=bank[0]), reads=[bank[1]], writes=[Sx[1]])
                        if j > 0:
                            pool.op(lambda e, Sx=Sx, Sp=Sp: e.tensor_copy(out=Sx[0][:, 0:3], in_=Sp[0][:, BLK:BLK + 3]), reads=[Sp[1]], writes=[Sx[1]])
                            if b == 0:
                                conv(Sx, 3, BLK - 1, xc[0][:, 0:BLK - 1])
                                pool.op(lambda e, Sx=Sx: e.tensor_copy(out=first[0][:, 0:1], in_=Sx[0][:, 3:4]), reads=[Sx[1]], writes=[first[1]])
                            else:
                                conv(Sx, 2, BLK, xc[0][:, b * BLK - 1:(b + 1) * BLK - 1])
                            if b == NB - 1:
                                pool.op(lambda e, Sx=Sx: e.tensor_copy(out=Sx[0][:, BLK + 3:BLK + 4], in_=first[0][:, 0:1]), reads=[first[1]], writes=[Sx[1]])
                                conv(Sx, BLK + 2, 1, xc[0][:, LV - 1:LV])
                        if j > 0 and b * BLK < LO:
                            bank = ps[zi % 2]; zi += 1
                            proj(bank, wyr, 128, 0, hb, n=QB)
                            act.op(lambda e, b=b, bank=bank: e.activation(out=gy[0][:, b * BLK:b * BLK + QB], in_=bank[0][:, 0:QB], func=AF.Gelu),
                                   reads=[bank[1]], writes=[gy[1]])
                    ar.pop()
                    ar.push()
                    NSET = 4
                    xbb = [ar.bf16(BLK, f"xbb{i}") for i in range(NSET)]
                    rr = [ar.f32(BLK, f"rr{i}") for i in range(NSET)]
                    aa = [ar.f32(BLK, f"aa{i}") for i in range(NSET)]
                    ssq = [ar.f32(BLK, f"ssq{i}") for i in range(NSET)]
                    ii = [ar.f32(BLK, f"ii{i}") for i in range(NSET)]
                    uu = [ar.f32(BLK, f"uu{i}") for i in range(NSET)]
                    mk = [ar.f32(BLK, f"mk{i}") for i in range(NSET)]
                    hh_ = [ar.f32(BLK, f"hh{i}") for i in range(NSET)]
                    nob = LO // BLK
                    border = [list(range(nob, NB)) + list(range(nob)), list(range(NB - 1, -1, -1))]
                    for d in range(2):
                        dve.op(lambda e, d=d: e.memset(stt[d][0][:, 0:1], 0.0), writes=[stt[d][1]])
                    for j in range(NB):
                        bl = [border[0][j], border[1][j]]
                        st_ = [0 * 2 + j % 2, 1 * 2 + j % 2]
                        xs = [xc[0][:, bl[d] * BLK:(bl[d] + 1) * BLK] for d in range(2)]
                        for d in range(2):
                            i = st_[d]; t0 = V0 + bl[d] * BLK
                            sp.dma(lambda e, i=i, t0=t0, d=d: e.dma_start(out=mk[i][0], in_=lmask[d:d + 1, t0:t0 + BLK].to_broadcast([128, BLK])), writes=[mk[i][1]])
                        for d in range(2):
                            i = st_[d]
                            pool.op(lambda e, i=i, d=d: e.tensor_copy(out=xbb[i][0], in_=xs[d]), reads=[xc[1]], writes=[xbb[i][1]])
                        for d in range(2):
                            i = st_[d]; ba = ps[2 + d * 2]; bx = ps[3 + d * 2]
                            pe.op(lambda e, i=i, d=d, ba=ba: e.matmul(ba[0], lhsT=gm[0 * 2 + d][0], rhs=xbb[i][0], start=True, stop=True),
                                  reads=[gm[d][1], xbb[i][1]], writes=[ba[1]])
                            pe.op(lambda e, i=i, d=d, bx=bx: e.matmul(bx[0], lhsT=gm[1 * 2 + d][0], rhs=xbb[i][0], start=True, stop=True),
                                  reads=[gm[2 + d][1], xbb[i][1]], writes=[bx[1]])
                        for d in range(2):
                            i = st_[d]; ba = ps[2 + d * 2]
                            act.op(lambda e, i=i, d=d, ba=ba: e.activation(out=rr[i][0], in_=ba[0], func=AF.Tanh, bias=hvec[0][:, d * 8 + c:d * 8 + c + 1], scale=0.5),
                                   reads=[ba[1], hvec[1]], writes=[rr[i][1]])
                        for d in range(2):
                            i = st_[d]; bx = ps[3 + d * 2]
                            act.op(lambda e, i=i, d=d, bx=bx: e.activation(out=ii[i][0], in_=bx[0], func=AF.Tanh, bias=hvec[0][:, 16 + d * 8 + c:16 + d * 8 + c + 1], scale=0.5),
                                   reads=[bx[1], hvec[1]], writes=[ii[i][1]])
                        for d in range(2):
                            i = st_[d]
                            act.op(lambda e, i=i, d=d: e.activation(out=aa[i][0], in_=rr[i][0], func=AF.Exp, bias=hvec[0][:, 32 + d * 8 + c:32 + d * 8 + c + 1], scale=hvec[0][:, 32 + d * 8 + c:32 + d * 8 + c + 1]),
                                   reads=[rr[i][1], hvec[1]], writes=[aa[i][1]])
                        for d in range(2):
                            i = st_[d]
                            act.op(lambda e, i=i: e.activation(out=ssq[i][0], in_=aa[i][0], func=AF.Square), reads=[aa[i][1]], writes=[ssq[i][1]])
                        for d in range(2):
                            i = st_[d]
                            dve.op(lambda e, i=i, d=d: e.scalar_tensor_tensor(out=uu[i][0], in0=ii[i][0], scalar=1.0, in1=xs[d], op0=ALU.add, op1=ALU.mult),
                                   reads=[ii[i][1], xc[1]], writes=[uu[i][1]])
                        for d in range(2):
                            i = st_[d]
                            act.op(lambda e, i=i: e.activation(out=ssq[i][0], in_=ssq[i][0], func=AF.Sqrt, bias=cst[0][:, 2:3], scale=-0.25),
                                   reads=[ssq[i][1], cst[1]], writes=[ssq[i][1]])
                        for d in range(2):
                            i = st_[d]
                            pool.op(lambda e, i=i: e.tensor_tensor(out=aa[i][0], in0=aa[i][0], in1=mk[i][0], op=ALU.mult), reads=[aa[i][1], mk[i][1]], writes=[aa[i][1]])
                        for d in range(2):
                            i = st_[d]
                            dve.op(lambda e, i=i: e.tensor_tensor(out=uu[i][0], in0=uu[i][0], in1=ssq[i][0], op=ALU.mult), reads=[uu[i][1], ssq[i][1]], writes=[uu[i][1]])
                        for d in range(2):
                            i = st_[d]; b_ = bl[d]
                            if b_ * BLK < LO:
                                hdst = hown[d][0][:, b_ * BLK:(b_ + 1) * BLK]; hres = hown[d][1]
                            else:
                                hdst = hh_[i][0]; hres = hh_[i][1]
                            if d == 0:
                                dve.op(lambda e, i=i, hdst=hdst: e.tensor_tensor_scan(out=hdst, data0=aa[i][0], data1=uu[i][0], initial=stt[0][0][:, 0:1], op0=ALU.mult, op1=ALU.add),
                                       reads=[aa[i][1], uu[i][1], stt[0][1]], writes=[hres])
                                dve.op(lambda e, hdst=hdst: e.tensor_copy(out=stt[0][0][:, 0:1], in_=hdst[:, BLK - 1:BLK]), reads=[hres], writes=[stt[0][1]])
                            else:
                                dve.op(lambda e, i=i, hdst=hdst: e.tensor_tensor_scan(out=hdst[:, ::-1], data0=aa[i][0][:, ::-1], data1=uu[i][0][:, ::-1], initial=stt[1][0][:, 0:1], op0=ALU.mult, op1=ALU.add),
                                       reads=[aa[i][1], uu[i][1], stt[1][1]], writes=[hres])
                                dve.op(lambda e, hdst=hdst: e.tensor_copy(out=stt[1][0][:, 0:1], in_=hdst[:, 0:1]), reads=[hres], writes=[stt[1][1]])
                    for ob_ in range(NOB):
                        lt = lo_t[ob_ % 2]
                        sl = slice(ob_ * QB, (ob_ + 1) * QB)
                        pool.op(lambda e, sl=sl: e.tensor_tensor(out=hown[0][0][:, sl], in0=hown[0][0][:, sl], in1=hown[1][0][:, sl], op=ALU.add),
                                reads=[hown[0][1], hown[1][1]], writes=[hown[0][1]])
                        pool.op(lambda e, sl=sl, lt=lt: e.tensor_tensor(out=lt[0][:, 0:QB], in0=hown[0][0][:, sl], in1=gy[0][:, sl], op=ALU.mult),
                                reads=[hown[0][1], gy[1]], writes=[lt[1]])
                        t0 = O0 + ob_ * QB
                        sp.dma(lambda e, lt=lt, t0=t0: e.dma_start(out=mixT[8 + c, :, t0:t0 + QB], in_=lt[0][:, 0:QB]), reads=[lt[1]])
                    ar.pop()
                    ar.pop()
            fw.barrier()

        if "P3a" in phases:
            ar.push()
            wo = ar.bf16(16 * D, "wo")
            wst = [ar.f32(D, f"wost{i}") for i in range(2)]
            for f in range(16):
                a = wst[f % 2]
                row0 = f * 128 if f < 8 else 1024 + (f - 8) * 128
                gcol = vecs[0][:, VGA + f:VGA + f + 1] if f < 8 else vecs[0][:, VGL + f - 8:VGL + f - 7]
                sp.dma(lambda e, a=a, row0=row0: e.dma_start(out=a[0], in_=w_out[row0:row0 + 128, :]), writes=[a[1]])
                q = dve if f % 2 == 0 else pool
                q.op(lambda e, a=a, f=f, gcol=gcol: e.tensor_scalar(out=wo[0][:, f * D:(f + 1) * D], in0=a[0], scalar1=gcol, scalar2=None, op0=ALU.mult),
                     reads=[a[1], vecs[1]], writes=[wo[1]])
            MB = 256 if NOWN % 512 else 512
            mts = [ar.bf16(16 * MB, f"mt{i}") for i in range(2)]
            sqs = [ar.bf16(16 * MB, f"msq{i}") for i in range(2)]
            xo = [ar.f32(D, f"xo{i}") for i in range(2)]
            x2 = [ar.f32(D, f"x2{i}") for i in range(2)]
            stat = [ar.f32(4, f"st{i}") for i in range(2)]
            own_rows = []
            for s in range(len(cfg.seq_lens)):
                for t in range(0, cfg.lo[s], 128):
                    own_rows.append(cfg.voff[s] + t)
            ntile = NOWN // 128
            tpb = MB // 128
            for i in range(ntile):
                mt = mts[(i // tpb) % 2]; msq = sqs[(i // tpb) % 2]
                if i % tpb == 0:
                    t0 = i * 128
                    sp.dma(lambda e, mt=mt, t0=t0: e.dma_start(out=mt[0].rearrange("p (k t) -> p k t", k=16),
                                                                 in_=mixT[:, :, t0:t0 + MB].rearrange("k p t -> p k t")), writes=[mt[1]])
                    pool.op(lambda e, mt=mt, msq=msq: e.tensor_tensor(out=msq[0], in0=mt[0], in1=mt[0], op=ALU.mult), reads=[mt[1]], writes=[msq[1]])
                xi = xo[i % 2]; xr2 = x2[i % 2]; stt = stat[i % 2]
                r0 = own_rows[i]
                sp.dma(lambda e, xi=xi, r0=r0: e.dma_start(out=xi[0], in_=xv[r0:r0 + 128, :]), writes=[xi[1]])
                tsl = (i % tpb) * 128
                for part in range(2):
                    for f8 in range(8):
                        f = part * 8 + f8
                        pe.op(lambda e, f=f, f8=f8, part=part, msq=msq, tsl=tsl: e.matmul(ps[6 + part][0][:, 0:1], lhsT=msq[0][:, f * MB + tsl:f * MB + tsl + 128], rhs=onesb[0][:, 0:1], start=(f8 == 0), stop=(f8 == 7)),
                              reads=[msq[1], onesb[1]], writes=[ps[6 + part][1]])
                    act.op(lambda e, part=part, stt=stt: e.activation(out=stt[0][:, part:part + 1], in_=ps[6 + part][0][:, 0:1], func=AF.Sqrt, bias=cst[0][:, 0:1], scale=1.0 / 1024.0),
                           reads=[ps[6 + part][1], cst[1]], writes=[stt[1]])
                dve.op(lambda e, stt=stt: e.reciprocal(out=stt[0][:, 2:4], in_=stt[0][:, 0:2]), reads=[stt[1]], writes=[stt[1]])
                for qd in range(4):
                    ba = ps[(qd % 2) * 2]; bl = ps[(qd % 2) * 2 + 1]
                    for part, bank in ((0, ba), (1, bl)):
                        for f8 in range(8):
                            f = part * 8 + f8
                            pe.op(lambda e, f=f, f8=f8, bank=bank, mt=mt, tsl=tsl, qd=qd: e.matmul(bank[0], lhsT=mt[0][:, f * MB + tsl:f * MB + tsl + 128], rhs=wo[0][:, f * D + qd * 512:f * D + (qd + 1) * 512], start=(f8 == 0), stop=(f8 == 7)),
                                  reads=[mt[1], wo[1]], writes=[bank[1]])
                    csl = slice(qd * 512, (qd + 1) * 512)
                    dve.op(lambda e, ba=ba, csl=csl, stt=stt, xi=xi, xr2=xr2: e.scalar_tensor_tensor(out=xr2[0][:, csl], in0=ba[0], scalar=stt[0][:, 2:3], in1=xi[0][:, csl], op0=ALU.mult, op1=ALU.add),
                           reads=[ba[1], stt[1], xi[1]], writes=[xr2[1]])
                    dve.op(lambda e, bl=bl, csl=csl, stt=stt, xr2=xr2: e.scalar_tensor_tensor(out=xr2[0][:, csl], in0=bl[0], scalar=stt[0][:, 3:4], in1=xr2[0][:, csl], op0=ALU.mult, op1=ALU.add),
                           reads=[bl[1], stt[1], xr2[1]], writes=[xr2[1]])
                sp.dma(lambda e, xr2=xr2, i=i: e.dma_start(out=x2d[i * 128:(i + 1) * 128, :], in_=xr2[0]), reads=[xr2[1]])
            ar.pop()
            fw.barrier()

        if "P3b" in phases:
            ar.push()
            wqy = ar.bf16(16 * 1024, "wqy")
            rkb = ar.bf16(8 * 256, "rkb")
            gf = ar.f32(D, "gffn")
            ar.push()
            wst = [ar.f32(1024, f"wqst{i}") for i in range(2)]
            for kc in range(16):
                a_ = wst[kc % 2]
                sp.dma(lambda e, a_=a_, kc=kc: e.dma_start(out=a_[0], in_=w_query[kc * 128:(kc + 1) * 128, :]), writes=[a_[1]])
                q = dve if kc % 2 == 0 else pool
                q.op(lambda e, a_=a_, kc=kc: e.tensor_copy(out=wqy[0][:, kc * 1024:(kc + 1) * 1024], in_=a_[0]), reads=[a_[1]], writes=[wqy[1]])
            for h in range(8):
                a_ = wst[h % 2]
                sp.dma(lambda e, a_=a_, h=h: e.dma_start(out=a_[0][:, 0:256], in_=rk_d[h]), writes=[a_[1]])
                dve.op(lambda e, a_=a_, h=h: e.tensor_copy(out=rkb[0][:, h * 256:(h + 1) * 256], in_=a_[0][:, 0:256]), reads=[a_[1]], writes=[rkb[1]])
            sp.dma(lambda e: e.dma_start(out=gf[0], in_=gffn_d.to_broadcast([128, D])), writes=[gf[1]])
            ar.pop()
            jA = ar.bf16(D, "pjA")
            NPR = 3
            prod = [ar.bf16(D, f"pprod{i}") for i in range(NPR)]
            tslot = [Res(f"tslot{i}") for i in range(8)]
            xnT = ar.bf16(16 * 128, "pxnT")
            qT = ar.bf16(8 * 128, "pqT")
            sc = ar.f32(2048, "psc")
            tmp = ar.f32(2048, "ptmp")
            cid = ar.f32(2048, "pcid")
            vals = ar.f32(256, "pvals")
            idxu = ar.u32(256, "pidx")
            idxf = ar.f32(256, "pidxf")
            best = ar.f32(128, "pbest")
            gsum = ar.f32(8, "pgsum")
            negm = ar.f32(8, "pnegm")
            eidf = ar.f32(128, "peidf")
            stat = ar.f32(4, "pstat")
            x2 = [ar.f32(D, f"px2{i}") for i in range(2)]
            xnb = [ar.bf16(D, f"pxnb{i}") for i in range(2)]
            eidu = [ar.u32(128, f"peidu{i}") for i in range(2)]
            gate = [ar.f32(128, f"pgate{i}") for i in range(2)]
            hid = [ar.f32(128, f"phid{i}") for i in range(2)]
            gel = [ar.f32(128, f"pgel{i}") for i in range(2)]
            NDG = 8
            diag = [ar.bf16(128, f"pdiag{i}") for i in range(NDG)]
            NU = 9
            U = [ar.bf16(2 * D, f"U{i}") for i in range(NU)]
            ntile = NOWN // 128
            ebf_rows = ebf

            def prep(i):
                par = i % 2
                xt = x2[par]; xb_ = xnb[par]
                sp.dma(lambda e: e.dma_start(out=xt[0], in_=x2d[i * 128:(i + 1) * 128, :]), writes=[xt[1]])
                act.op(lambda e: e.activation(out=jA[0], in_=xt[0], func=AF.Square, accum_out=stat[0][:, 0:1]), reads=[xt[1]], writes=[jA[1], stat[1]])
                act.op(lambda e: e.activation(out=stat[0][:, 1:2], in_=stat[0][:, 0:1], func=AF.Sqrt, bias=cst[0][:, 0:1], scale=1.0 / D), reads=[stat[1], cst[1]], writes=[stat[1]])
                dve.op(lambda e: e.reciprocal(out=stat[0][:, 2:3], in_=stat[0][:, 1:2]), reads=[stat[1]], writes=[stat[1]])
                dve.op(lambda e: e.scalar_tensor_tensor(out=xb_[0], in0=xt[0], scalar=stat[0][:, 2:3], in1=gf[0], op0=ALU.mult, op1=ALU.mult),
                       reads=[xt[1], stat[1], gf[1]], writes=[xb_[1]])
                yield
                for half in range(2):
                    bank = ps[half]
                    pv = bank[0].bitcast(BF16)
                    for j in range(8):
                        kc = half * 8 + j
                        pe.op(lambda e, pv=pv, j=j, kc=kc: e.transpose(out=pv[:, j * 128:(j + 1) * 128], in_=xb_[0][:, kc * 128:(kc + 1) * 128], identity=idb[0]),
                              reads=[xb_[1], idb[1]], writes=[bank[1]])
                    act.op(lambda e, pv=pv, half=half: e.copy(out=xnT[0][:, half * 1024:(half + 1) * 1024], in_=pv), reads=[bank[1]], writes=[xnT[1]])
                for hq in range(2):
                    bank = ps[2 + hq]
                    for h4 in range(4):
                        h = hq * 4 + h4
                        for kc in range(16):
                            pe.op(lambda e, bank=bank, h=h, h4=h4, kc=kc: e.matmul(bank[0][:, h4 * 128:(h4 + 1) * 128], lhsT=wqy[0][:, kc * 1024 + h * 128:kc * 1024 + (h + 1) * 128],
                                                                                     rhs=xnT[0][:, kc * 128:(kc + 1) * 128], start=(kc == 0), stop=(kc == 15)),
                                  reads=[wqy[1], xnT[1]], writes=[bank[1]])
                    act.op(lambda e, bank=bank, hq=hq: e.copy(out=qT[0][:, hq * 512:(hq + 1) * 512], in_=bank[0]), reads=[bank[1]], writes=[qT[1]])
                yield
                for h in range(8):
                    bank = ps[h // 2]
                    pe.op(lambda e, bank=bank, h=h: e.matmul(bank[0][:, (h % 2) * 256:(h % 2 + 1) * 256], lhsT=qT[0][:, h * 128:(h + 1) * 128], rhs=rkb[0][:, h * 256:(h + 1) * 256], start=True, stop=True),
                          reads=[qT[1], rkb[1]], writes=[bank[1]])
                for bq in range(4):
                    act.op(lambda e, bq=bq: e.copy(out=sc[0][:, bq * 512:(bq + 1) * 512], in_=ps[bq][0]), reads=[ps[bq][1]], writes=[sc[1]])
                yield
                sgs = [sc[0][:, gq * 128:(gq + 1) * 128] for gq in range(16)]
                tgs = [tmp[0][:, gq * 128:(gq + 1) * 128] for gq in range(16)]
                for gq in range(16):
                    dve.op(lambda e, gq=gq: e.max(out=vals[0][:, gq * 16:gq * 16 + 8], in_=sgs[gq]), reads=[sc[1]],
                           writes=([vals[1]] if gq == 0 else []), ww=([] if gq == 0 else [vals[1]]))
                for gq in range(16):
                    dve.op(lambda e, gq=gq: e.match_replace(out=tgs[gq], in_to_replace=vals[0][:, gq * 16:gq * 16 + 8], in_values=sgs[gq], imm_value=-1e30),
                           reads=[sc[1], vals[1]], writes=([tmp[1]] if gq == 0 else []), ww=([] if gq == 0 else [tmp[1]]))
                yield
                for gq in range(16):
                    dve.op(lambda e, gq=gq: e.max(out=vals[0][:, gq * 16 + 8:gq * 16 + 16], in_=tgs[gq]), reads=[tmp[1]], ww=[vals[1]])
                for gq in range(16):
                    dve.op(lambda e, gq=gq: e.max_index(out=idxu[0][:, gq * 16:gq * 16 + 8], in_max=vals[0][:, gq * 16:gq * 16 + 8], in_values=sgs[gq]), reads=[sc[1], vals[1]],
                           writes=([idxu[1]] if gq == 0 else []), ww=([] if gq == 0 else [idxu[1]]))
                yield
                for gq in range(16):
                    dve.op(lambda e, gq=gq: e.max_index(out=idxu[0][:, gq * 16 + 8:gq * 16 + 16], in_max=vals[0][:, gq * 16 + 8:gq * 16 + 16], in_values=sgs[gq]), reads=[sc[1], vals[1]], ww=[idxu[1]])
                dve.op(lambda e: e.tensor_copy(out=idxf[0], in_=idxu[0]), reads=[idxu[1]], writes=[idxf[1]])
                v4 = vals[0].rearrange("p (h c k) -> p h c k", h=8, c=2)
                i4 = idxf[0].rearrange("p (h c k) -> p h c k", h=8, c=2)
                c4 = sc[0].rearrange("p (h a b) -> p h a b", h=8, a=16)
                d4 = cid[0].rearrange("p (h a b) -> p h a b", h=8, a=16)
                dve.op(lambda e: e.tensor_scalar(out=i4[:, :, 0, :], in0=i4[:, :, 0, :], scalar1=128.0, scalar2=None, op0=ALU.mult), reads=[idxf[1]], writes=[idxf[1]])
                dve.op(lambda e: e.tensor_tensor(out=c4, in0=v4[:, :, 0, :].unsqueeze(3).to_broadcast([128, 8, 16, 16]),
                                                 in1=v4[:, :, 1, :].unsqueeze(2).to_broadcast([128, 8, 16, 16]), op=ALU.add),
                       reads=[vals[1]], writes=[sc[1]])
                dve.op(lambda e: e.tensor_tensor(out=d4, in0=i4[:, :, 0, :].unsqueeze(3).to_broadcast([128, 8, 16, 16]),
                                                 in1=i4[:, :, 1, :].unsqueeze(2).to_broadcast([128, 8, 16, 16]), op=ALU.add),
                       reads=[idxf[1]], writes=[cid[1]])
                yield
                cgs = [sc[0][:, h * 256:(h + 1) * 256] for h in range(8)]
                ths = [tmp[0][:, h * 256:(h + 1) * 256] for h in range(8)]
                for h in range(8):
                    dve.op(lambda e, h=h: e.max(out=best[0][:, h * 16:h * 16 + 8], in_=cgs[h]), reads=[sc[1]],
                           writes=([best[1]] if h == 0 else []), ww=([] if h == 0 else [best[1]]))
                for h in range(8):
                    dve.op(lambda e, h=h: e.match_replace(out=ths[h], in_to_replace=best[0][:, h * 16:h * 16 + 8], in_values=cgs[h], imm_value=-1e30),
                           reads=[sc[1], best[1]], writes=([tmp[1]] if h == 0 else []), ww=([] if h == 0 else [tmp[1]]))
                for h in range(8):
                    dve.op(lambda e, h=h: e.max(out=best[0][:, h * 16 + 8:h * 16 + 16], in_=ths[h]), reads=[tmp[1]], ww=[best[1]])
                yield
                gt = gate[par]
                dve.op(lambda e: e.tensor_scalar(out=negm[0], in0=best[0].rearrange("p (h k) -> p h k", h=8)[:, :, 0], scalar1=-1.0, scalar2=None, op0=ALU.mult),
                       reads=[best[1]], writes=[negm[1]])
                for h in range(8):
                    act.op(lambda e, h=h: e.activation(out=gt[0][:, h * 16:(h + 1) * 16], in_=best[0][:, h * 16:(h + 1) * 16], func=AF.Exp, bias=negm[0][:, h:h + 1], accum_out=gsum[0][:, h:h + 1]),
                           reads=[best[1], negm[1]], writes=([gt[1], gsum[1]] if h == 0 else []), ww=([] if h == 0 else [gt[1], gsum[1]]))
                dve.op(lambda e: e.reciprocal(out=gsum[0], in_=gsum[0]), reads=[gsum[1]], writes=[gsum[1]])
                dve.op(lambda e: e.tensor_tensor(out=gt[0].rearrange("p (h k) -> p h k", h=8), in0=gt[0].rearrange("p (h k) -> p h k", h=8),
                                                 in1=gsum[0].unsqueeze(2).to_broadcast([128, 8, 16]), op=ALU.mult), reads=[gt[1], gsum[1]], writes=[gt[1]])
                for hk in range(128):
                    h = hk // 16
                    first = (hk == 0)
                    dve.op(lambda e, h=h, hk=hk: e.scalar_tensor_tensor(out=tmp[0][:, (hk % 8) * 256:(hk % 8 + 1) * 256], in0=sc[0][:, h * 256:(h + 1) * 256], scalar=best[0][:, hk:hk + 1],
                                                                      in1=cid[0][:, h * 256:(h + 1) * 256], op0=ALU.is_equal, op1=ALU.mult, accum_out=eidf[0][:, hk:hk + 1]),
                           reads=[sc[1], best[1], cid[1]], writes=([tmp[1], eidf[1]] if first else []) + [tslot[hk % 8]], ww=([] if first else [tmp[1], eidf[1]]))
                    if hk % 32 == 31:
                        yield
                dve.op(lambda e: e.tensor_scalar(out=eidf[0], in0=eidf[0], scalar1=float(NEXP - 1), scalar2=0.0, op0=ALU.min, op1=ALU.max), reads=[eidf[1]], writes=[eidf[1]])
                dve.op(lambda e: e.tensor_copy(out=eidu[par][0], in_=eidf[0]), reads=[eidf[1]], writes=[eidu[par][1]])

            ucount = [0]
            dcount = [0]
            jcount = [0]

            def main(i, gen):
                par = i % 2
                xt = x2[par]; xb_ = xnb[par]; hd = hid[par]; ge = gel[par]; gt = gate[par]
                for g4 in range(32):
                    us = []
                    for k4 in range(4):
                        hk = g4 * 4 + k4
                        u = U[ucount[0] % NU]
                        ucount[0] += 1
                        pool.dma(lambda e, u=u, hk=hk: e.indirect_dma_start(out=u[0], out_offset=None, in_=ebf_rows,
                                                                            in_offset=bass.IndirectOffsetOnAxis(ap=eidu[par][0][:, hk:hk + 1], axis=0)),
                                 reads=[eidu[par][1]], writes=[u[1]])
                        us.append(u)
                    for k4 in range(4):
                        hk = g4 * 4 + k4
                        u = us[k4]
                        strong = (hk == 0)
                        pr = prod[jcount[0] % NPR]
                        jcount[0] += 1
                        dve.op(lambda e, u=u, pr=pr: e.tensor_tensor(out=pr[0], in0=u[0][:, 0:D], in1=xb_[0], op=ALU.mult), reads=[u[1], xb_[1]], writes=[pr[1]])
                        act.op(lambda e, pr=pr, hk=hk: e.activation(out=pr[0], in_=pr[0], func=AF.Copy, accum_out=hd[0][:, hk:hk + 1]),
                               reads=[pr[1]], writes=[pr[1]] + ([hd[1]] if strong else []), ww=([] if strong else [hd[1]]))
                    sl = slice(g4 * 4, g4 * 4 + 4)
                    act.op(lambda e, sl=sl: e.activation(out=ge[0][:, sl], in_=hd[0][:, sl], func=AF.Gelu), reads=[hd[1]],
                           writes=([ge[1]] if g4 == 0 else []), ww=([] if g4 == 0 else [ge[1]]))
                    for k4 in range(4):
                        hk = g4 * 4 + k4
                        u = us[k4]
                        dg = diag[dcount[0] % NDG]
                        dcount[0] += 1
                        dve.op(lambda e, dg=dg, hk=hk: e.tensor_scalar(out=dg[0], in0=idb[0], scalar1=ge[0][:, hk:hk + 1], scalar2=gt[0][:, hk:hk + 1], op0=ALU.mult, op1=ALU.mult),
                               reads=[idb[1], ge[1], gt[1]], writes=[dg[1]])
                        for q4 in range(4):
                            pe.op(lambda e, dg=dg, u=u, q4=q4, hk=hk: e.matmul(ps[4 + q4][0], lhsT=dg[0], rhs=u[0][:, D + q4 * 512:D + (q4 + 1) * 512], start=(hk == 0), stop=(hk == 127)),
                                  reads=[dg[1], u[1]], writes=[ps[4 + q4][1]])
                    if gen is not None and g4 % 2 == 1:
                        next(gen, None)
                if gen is not None:
                    for _ in gen:
                        pass
                for q4 in range(4):
                    csl = slice(q4 * 512, (q4 + 1) * 512)
                    dve.op(lambda e, q4=q4, csl=csl: e.tensor_tensor(out=xt[0][:, csl], in0=ps[4 + q4][0], in1=xt[0][:, csl], op=ALU.add),
                           reads=[ps[4 + q4][1], xt[1]], writes=[xt[1]])
                sp.dma(lambda e: e.dma_start(out=y[i * 128:(i + 1) * 128, :], in_=xt[0]), reads=[xt[1]])

            for _ in prep(0):
                pass
            for i in range(ntile):
                gen = prep(i + 1) if i + 1 < ntile else None
                main(i, gen)
            ar.pop()

        fw.emit(stack)
    return nc


def _rope_tables(L):
    half = DH // 2
    inv = (10000.0 ** (-np.arange(0, half, 2, dtype=np.float32) / half)).astype(np.float32)
    t = np.arange(L)
    row = (t // 64).astype(np.float32)
    col = (t % 64).astype(np.float32)
    ang_r = row[None, :] * inv[:, None]
    ang_c = col[None, :] * inv[:, None]
    ang = np.concatenate([ang_r, ang_r, ang_c, ang_c], axis=0).astype(np.float32)
    return np.cos(ang).astype(np.float32), np.sin(ang).astype(np.float32)


def _prot():
    P = np.zeros((128, 128), np.float32)
    for base in (0, 64):
        for j in range(32):
            P[base + j, base + j + 32] = -1.0
            P[base + j + 32, base + j] = 1.0
    return np.ascontiguousarray(P.T)


def prep_inputs(cfg, inp):
    f = np.float32
    seqs = [np.asarray(inp["x_prompt"], f)[0]] + [np.asarray(inp["x_sample"], f)[b] for b in range(np.asarray(inp["x_sample"]).shape[0])]
    seqs = seqs[:len(cfg.seq_lens)]
    vec = np.zeros((128, NVEC), f)
    vec[:, VG_MIX:VG_MIX + 16] = np.asarray(inp["g_mix"], f)[0].reshape(16, 128).T
    vec[:, VG_Q] = np.asarray(inp["g_q"], f)[0]
    vec[:, VG_K] = np.asarray(inp["g_k"], f)[0]
    vec[:, VCONVW:VCONVW + 32] = np.asarray(inp["conv_w"], f)[0].reshape(4, 8, 128).transpose(2, 0, 1).reshape(128, 32)
    vec[:, VCONVB:VCONVB + 8] = np.asarray(inp["conv_b"], f)[0].reshape(8, 128).T
    vec[:, VBA:VBA + 16] = np.asarray(inp["b_gate_a"], f)[0].reshape(2, 8, 128).transpose(2, 0, 1).reshape(128, 16)
    vec[:, VBX:VBX + 16] = np.asarray(inp["b_gate_x"], f)[0].reshape(2, 8, 128).transpose(2, 0, 1).reshape(128, 16)
    vec[:, VLAM:VLAM + 16] = np.asarray(inp["lru_lambda"], f)[0].reshape(2, 8, 128).transpose(2, 0, 1).reshape(128, 16)
    vec[:, VGA:VGA + 8] = np.asarray(inp["g_attn_out"], f)[0].reshape(8, 128).T
    vec[:, VGL:VGL + 8] = np.asarray(inp["g_lru_out"], f)[0].reshape(8, 128).T
    wg = np.zeros((32, 128, 128), f)
    for gi, name in enumerate(("w_gate_a", "w_gate_x")):
        w = np.asarray(inp[name], f)[0]
        for d in range(2):
            for c in range(8):
                m = wg[(gi * 2 + d) * 8 + c]
                m[0:64, 0:64] = w[d, 2 * c]
                m[64:128, 64:128] = w[d, 2 * c + 1]
    sk = np.asarray(inp["sub_keys"], f)[0]
    rk = np.zeros((8, 128, 256), f)
    for h in range(8):
        rk[h, 0:64, 0:128] = sk[h, 0].T
        rk[h, 64:128, 128:256] = sk[h, 1].T
    shared = {
        "vecs": vec, "w_in": np.ascontiguousarray(np.asarray(inp["w_in"], f)[0]),
        "w_out": np.ascontiguousarray(np.asarray(inp["w_out"], f)[0]),
        "w_query": np.ascontiguousarray(np.asarray(inp["w_query"], f)[0]),
        "wg": wg, "rk": rk, "gffn": np.ascontiguousarray(np.asarray(inp["g_ffn"], f)[0].reshape(1, D)),
        "ident": np.eye(128, dtype=f), "prot": _prot(),
        "edown": np.ascontiguousarray(np.asarray(inp["expert_down"], f)[0]),
        "eup": np.ascontiguousarray(np.asarray(inp["expert_up"], f)[0]),
    }
    tabs = [_rope_tables(L) for L in cfg.seq_lens]
    maps = []
    for c in range(cfg.ncores):
        xv = np.zeros((cfg.nv, D), f)
        csv = np.zeros((2, 128, cfg.nv), f)
        csv[0] = 1.0
        kmv = np.zeros((cfg.nv,), f)
        lm = np.ones((2, cfg.nv), f)
        for s, L in enumerate(cfg.seq_lens):
            lo = cfg.lo[s]; o0 = c * lo; v0 = cfg.voff[s]
            n_after = L - o0 - lo
            idx = np.concatenate([np.arange(o0, L), np.arange(0, o0)])
            pos = np.concatenate([np.arange(0, L - o0), np.arange(L - o0 + PAD, L + PAD)])
            xv[v0 + pos] = seqs[s][idx]
            csv[0][:, v0 + pos] = tabs[s][0][:, idx]
            csv[1][:, v0 + pos] = tabs[s][1][:, idx]
            kmv[v0 + L - o0:v0 + L - o0 + PAD] = -30000.0
            p0 = (L - o0 + PAD) if o0 > 0 else 0
            lm[0, v0 + p0] = 0.0
            lm[1, v0 + (L - o0 - 1)] = 0.0
        m = dict(shared)
        m["xv"] = xv
        m["cs"] = csv
        m["kmask"] = np.ascontiguousarray(kmv.reshape(cfg.nv // 128, 128).T)
        m["lmask"] = lm
        maps.append(m)
    return maps


def assemble(cfg, ys, inp):
    outs = []
    shapes = [np.asarray(inp["x_prompt"]).shape, np.asarray(inp["x_sample"]).shape]
    full = [np.zeros((L, D), np.float32) for L in cfg.seq_lens]
    for c in range(cfg.ncores):
        for s in range(len(cfg.seq_lens)):
            lo = cfg.lo[s]
            full[s][c * lo:(c + 1) * lo] = ys[c][cfg.ooff[s]:cfg.ooff[s] + lo]
    y_prompt = full[0].reshape(shapes[0])
    y_sample = np.stack(full[1:], axis=0).reshape(shapes[1])
    return y_prompt, y_sample


_CACHE = {}


def kernel(**inputs):
    cfg = Cfg()
    if "nc" not in _CACHE:
        _CACHE["nc"] = build_program(cfg)
    nc = _CACHE["nc"]
    maps = prep_inputs(cfg, inputs)
    res = run_bass_kernel_spmd(nc, maps, core_ids=list(range(cfg.ncores)))
    ys = [np.asarray(r["y"], np.float32) for r in res.results]
    return assemble(cfg, ys, inputs)
```

```python
import numpy as np
from contextlib import ExitStack
import concourse.bass as bass
import concourse.mybir as mybir
from concourse.bass_utils import run_bass_kernel_spmd

F32 = mybir.dt.float32
BF16 = mybir.dt.bfloat16
U32 = mybir.dt.uint32
AF = mybir.ActivationFunctionType
ALU = mybir.AluOpType

D = 2048
DH = 128
NQH = 8
NKV = 2
INW = 3584
LRUW = 1024
NEXP = 16384
EPS = 1e-6
PAD = 512
BLK = 512
ARENA_F = 52500

GROUP = 8000
NSEM_PER_ENG = 20
NDMA_SEM = 48
NSW_SEM = 12

VG_MIX = 0
VG_Q = 16
VG_K = 17
VCONVW = 18
VCONVB = 50
VBA = 58
VBX = 74
VLAM = 90
VGA = 106
VGL = 114
NVEC = 122


class Res:
    __slots__ = ("w", "rs", "name")

    def __init__(self, name=""):
        self.w = None
        self.rs = {}
        self.name = name


class _Rec:
    def __init__(self):
        self.call = None

    def __getattr__(self, name):
        def f(*a, **k):
            assert self.call is None
            self.call = (name, a, k)
            return self
        return f


def _record(fn):
    r = _Rec()
    fn(r)
    assert r.call is not None
    return r.call


class Q:
    def __init__(self, fw, name, skip_self):
        self.fw = fw
        self.name = name
        self.n = 0
        self.prog = []
        self.waited = {}
        self.skip_self = skip_self
        self.pending = []

    def ev_for(self, idx):
        g = (idx - 1) // GROUP
        return ((self.name, g), idx - g * GROUP)

    def _collect(self, reads, writes):
        deps = []
        for r in reads:
            if r.w is not None:
                deps.append(r.w)
        for w in writes:
            if w.w is not None:
                deps.append(w.w)
            deps.extend(w.rs.values())
        if self.pending:
            deps.extend(self.pending)
            self.pending = []
        return deps

    def _filter(self, deps):
        need = {}
        for (key, val) in deps:
            if self.skip_self and key[0] == self.name:
                continue
            if self.waited.get(key, 0) >= val:
                continue
            if need.get(key, 0) < val:
                need[key] = val
        for key, val in need.items():
            self.waited[key] = val
            if key[0] in ("pe", "act", "dve", "pool", "sp"):
                for g in range(key[1]):
                    self.waited[(key[0], g)] = GROUP
        return list(need.items())

    def op(self, fn, reads=(), writes=(), ww=()):
        waits = self._filter(self._collect(reads, writes))
        self.n += 1
        ev = self.ev_for(self.n)
        self.prog.append((waits, _record(fn), ev, 1))
        for r in reads:
            r.rs[self.name] = ev
        for w in writes:
            w.w = ev
            w.rs = {}
        for w in ww:
            w.w = ev
        return ev

    def dma(self, fn, reads=(), writes=()):
        fw = self.fw
        deps = self._collect(reads, writes)
        if self.name == "pool":
            j = NDMA_SEM - NSW_SEM + fw.sw_next % NSW_SEM
            fw.sw_next += 1
        else:
            j = fw.dma_next % (NDMA_SEM - NSW_SEM)
            fw.dma_next += 1
        key = ("dma", j)
        if fw.dma_cnt[j] > 0:
            deps.append((key, fw.dma_cnt[j]))
        waits = self._filter(deps)
        fw.dma_cnt[j] += 16
        ev = (key, fw.dma_cnt[j])
        self.prog.append((waits, _record(fn), ev, 16))
        for r in reads:
            r.rs[("dmar", j)] = ev
        for w in writes:
            w.w = ev
            w.rs = {}
        return ev


def _swdma(self, fn, slot, reads=(), writes=()):
    fw = self.fw
    waits = self._filter(self._collect(reads, writes))
    fw.sw_gen[slot] = fw.sw_gen.get(slot, 0) + 1
    ev = (("swdma", slot, fw.sw_gen[slot]), 16)
    self.prog.append((waits, ("__swdma__", slot, _record(fn)), ev, 16))
    for r in reads:
        r.rs[("swr", slot)] = ev
    for w in writes:
        w.w = ev
        w.rs = {}
    return ev


Q.swdma = _swdma


class FW:
    def __init__(self, nc):
        self.nc = nc
        self.pe = Q(self, "pe", True)
        self.act = Q(self, "act", False)
        self.dve = Q(self, "dve", False)
        self.pool = Q(self, "pool", False)
        self.sp = Q(self, "sp", False)
        self.queues = [self.pe, self.act, self.dve, self.pool, self.sp]
        self.dma_next = 0
        self.sw_next = 0
        self.dma_cnt = [0] * NDMA_SEM
        self.sw_gen = {}

    def all_events(self):
        evs = []
        for q in self.queues:
            if q.n > 0:
                evs.append(q.ev_for(q.n))
        for j in range(NDMA_SEM):
            if self.dma_cnt[j] > 0:
                evs.append((("dma", j), self.dma_cnt[j]))
        for slot, gen in self.sw_gen.items():
            evs.append((("swdma", slot, gen), 16))
        return evs

    def barrier(self):
        evs = self.all_events()
        for q in self.queues:
            q.pending.extend(evs)

    def emit(self, stack):
        nc = self.nc
        sems = {}
        for q in self.queues:
            ng = max(1, (q.n + GROUP - 1) // GROUP)
            assert ng <= NSEM_PER_ENG, (q.name, q.n)
            for g in range(ng):
                sems[(q.name, g)] = stack.enter_context(nc.semaphore(f"s_{q.name}{g}"))
        for j in range(NDMA_SEM):
            sems[("dma", j)] = stack.enter_context(nc.semaphore(f"s_dma{j}"))
        for slot in self.sw_gen:
            sems[("swdma", slot)] = stack.enter_context(nc.semaphore(f"s_sw{slot}"))

        def semof(key):
            return sems[key[:2]] if key[0] == "swdma" else sems[key]
        block = stack.enter_context(nc.Block())
        handles = {"pe": block.tensor, "act": block.scalar, "dve": block.vector,
                   "pool": block.gpsimd, "sp": block.sync}
        final = self.all_events()

        def make(q):
            def body(eng):
                for (waits, call, ev, inc) in q.prog:
                    for (key, val) in waits:
                        eng.wait_ge(semof(key), val)
                    if call[0] == "__swdma__":
                        if ev[0][2] > 1:
                            eng.wait_ge(semof(ev[0]), 16)
                        eng.sem_clear(semof(ev[0]))
                        call = call[2]
                    ins = getattr(eng, call[0])(*call[1], **call[2])
                    ins.then_inc(semof(ev[0]), inc)
                if q.name == "sp":
                    for (key, val) in final:
                        eng.wait_ge(semof(key), val)
            return body

        for q in self.queues:
            handles[q.name](make(q))


class Arena:
    def __init__(self, ap_f32, nfloats, fw=None):
        self.base = ap_f32
        self.n = nfloats
        self.off = 0
        self.marks = []
        self.fw = fw

    def push(self):
        self.marks.append(self.off)

    def pop(self):
        self.off = self.marks.pop()
        if self.fw is not None:
            self.fw.barrier()

    def f32(self, n, name=""):
        n8 = (n + 7) // 8 * 8
        assert self.off + n8 <= self.n, f"SBUF arena overflow at {name}: {self.off}+{n8}>{self.n}"
        ap = self.base[:, self.off:self.off + n]
        self.off += n8
        return ap, Res(name)

    def bf16(self, n, name=""):
        nf = (n + 1) // 2
        ap, r = self.f32(nf, name)
        return ap.bitcast(BF16)[:, 0:n], r

    def u32(self, n, name=""):
        ap, r = self.f32(n, name)
        return ap.bitcast(U32), r


class Cfg:
    def __init__(self, seq_lens=(16384, 8192, 8192), ncores=8):
        self.seq_lens = list(seq_lens)
        self.ncores = ncores
        self.lo = [L // ncores for L in self.seq_lens]
        self.lv = [L + PAD for L in self.seq_lens]
        self.voff = [0]
        for v in self.lv:
            self.voff.append(self.voff[-1] + v)
        self.nv = self.voff[-1]
        self.ooff = [0]
        for o in self.lo:
            self.ooff.append(self.ooff[-1] + o)
        self.nown = self.ooff[-1]
        for o in self.lo:
            assert o % BLK == 0, o


def build_program(cfg, phases=("W", "P0", "P1", "P2", "P3a", "P3b"), debug_outs=()):
    nc = bass.Bass("TRN2", target_bir_lowering=False)
    NV, NOWN = cfg.nv, cfg.nown

    def din(name, shape, dt=F32):
        return nc.dram_tensor(name, list(shape), dt, kind="ExternalInput").ap()

    def dscr(name, shape, dt):
        kind = "ExternalOutput" if name in debug_outs else "Internal"
        return nc.dram_tensor(name, list(shape), dt, kind=kind).ap()

    xv = din("xv", [NV, D])
    cs = din("cs", [2, 128, NV])
    kmask = din("kmask", [128, NV // 128])
    lmask = din("lmask", [2, NV])
    vecs_d = din("vecs", [128, NVEC])
    w_in = din("w_in", [D, INW])
    w_out = din("w_out", [D, D])
    w_query = din("w_query", [D, 1024])
    wg_d = din("wg", [32, 128, 128])
    rk_d = din("rk", [8, 128, 256])
    gffn_d = din("gffn", [1, D])
    ident_d = din("ident", [128, 128])
    prot_d = din("prot", [128, 128])
    edown = din("edown", [NEXP, D])
    eup = din("eup", [NEXP, D])
    y = nc.dram_tensor("y", [NOWN, D], F32, kind="ExternalOutput").ap()

    hT = dscr("hT", [NV // BLK, 128, 16 * BLK], BF16)
    winb = dscr("winb", [16, 128, INW], BF16)
    mixT = dscr("mixT", [16, 128, NOWN], BF16)
    x2d = dscr("x2d", [NOWN, D], F32)
    ebf = dscr("ebf", [NEXP, 2 * D], BF16)

    with ExitStack() as stack:
        arena_t = stack.enter_context(nc.sbuf_tensor("arena", [128, ARENA_F], F32))
        ps = []
        for i in range(8):
            p = stack.enter_context(nc.psum_tensor(f"ps{i}", [128, 512], F32))
            ps.append((p[:], Res(f"ps{i}")))
        fw = FW(nc)
        ar = Arena(arena_t[:], ARENA_F, fw)
        pe, act, dve, pool, sp = fw.pe, fw.act, fw.dve, fw.pool, fw.sp

        vecs = ar.f32(NVEC, "vecs")
        idf = ar.f32(128, "idf")
        idb = ar.bf16(128, "idb")
        onesb = ar.bf16(128, "onesb")
        onesf = ar.f32(128, "onesf")
        cst = ar.f32(8, "cst")
        rotq = ar.bf16(128, "rotq")
        rotk = ar.bf16(128, "rotk")
        cdec = ar.f32(16, "cdec")
        hvec = ar.f32(48, "hvec")
        halft = ar.f32(BLK, "halft")
        sp.dma(lambda e: e.dma_start(out=vecs[0], in_=vecs_d), writes=[vecs[1]])
        sp.dma(lambda e: e.dma_start(out=idf[0], in_=ident_d), writes=[idf[1]])
        dve.op(lambda e: e.tensor_copy(out=idb[0], in_=idf[0]), reads=[idf[1]], writes=[idb[1]])
        dve.op(lambda e: e.memset(onesb[0], 1.0), writes=[onesb[1]])
        dve.op(lambda e: e.memset(onesf[0], 1.0), writes=[onesf[1]])
        dve.op(lambda e: e.memset(cst[0][:, 0:1], EPS), writes=[cst[1]])
        dve.op(lambda e: e.memset(cst[0][:, 1:2], 1.0), writes=[cst[1]])
        dve.op(lambda e: e.memset(cst[0][:, 2:3], 0.25), writes=[cst[1]])
        ar.push()
        tmpc = ar.f32(128, "tmpc")
        sp.dma(lambda e: e.dma_start(out=tmpc[0], in_=prot_d), writes=[tmpc[1]])
        dve.op(lambda e: e.tensor_scalar(out=rotq[0], in0=tmpc[0], scalar1=vecs[0][:, VG_Q:VG_Q + 1], scalar2=None, op0=ALU.mult),
               reads=[tmpc[1], vecs[1]], writes=[rotq[1]])
        dve.op(lambda e: e.tensor_scalar(out=rotk[0], in0=tmpc[0], scalar1=vecs[0][:, VG_K:VG_K + 1], scalar2=None, op0=ALU.mult),
               reads=[tmpc[1], vecs[1]], writes=[rotk[1]])
        t_e = ar.f32(16, "t_e"); t_z = ar.f32(16, "t_z"); t_z2 = ar.f32(16, "t_z2"); t_p = ar.f32(16, "t_p")
        lam = vecs[0][:, VLAM:VLAM + 16]
        act.op(lambda e: e.activation(out=t_e[0], in_=lam, func=AF.Exp, scale=-1.0), reads=[vecs[1]], writes=[t_e[1]])
        dve.op(lambda e: e.tensor_scalar(out=t_z[0], in0=t_e[0], scalar1=2.0, scalar2=None, op0=ALU.add), reads=[t_e[1]], writes=[t_z[1]])
        dve.op(lambda e: e.reciprocal(out=t_z[0], in_=t_z[0]), reads=[t_z[1]], writes=[t_z[1]])
        dve.op(lambda e: e.tensor_tensor(out=t_z[0], in0=t_z[0], in1=t_e[0], op=ALU.mult), reads=[t_z[1], t_e[1]], writes=[t_z[1]])
        dve.op(lambda e: e.tensor_tensor(out=t_z2[0], in0=t_z[0], in1=t_z[0], op=ALU.mult), reads=[t_z[1]], writes=[t_z2[1]])
        dve.op(lambda e: e.tensor_scalar(out=t_p[0], in0=t_z2[0], scalar1=1.0 / 13.0, scalar2=1.0 / 11.0, op0=ALU.mult, op1=ALU.add),
               reads=[t_z2[1]], writes=[t_p[1]])
        for cf in (1.0 / 9.0, 1.0 / 7.0, 1.0 / 5.0, 1.0 / 3.0, 1.0):
            dve.op(lambda e: e.tensor_tensor(out=t_p[0], in0=t_p[0], in1=t_z2[0], op=ALU.mult), reads=[t_p[1], t_z2[1]], writes=[t_p[1]])
            dve.op(lambda e, cf=cf: e.tensor_scalar(out=t_p[0], in0=t_p[0], scalar1=cf, scalar2=None, op0=ALU.add), reads=[t_p[1]], writes=[t_p[1]])
        dve.op(lambda e: e.tensor_tensor(out=t_p[0], in0=t_p[0], in1=t_z[0], op=ALU.mult), reads=[t_p[1], t_z[1]], writes=[t_p[1]])
        dve.op(lambda e: e.tensor_scalar(out=cdec[0], in0=t_p[0], scalar1=-16.0, scalar2=None, op0=ALU.mult), reads=[t_p[1]], writes=[cdec[1]])
        dve.op(lambda e: e.tensor_scalar(out=hvec[0][:, 0:32], in0=vecs[0][:, VBA:VBA + 32], scalar1=0.5, scalar2=None, op0=ALU.mult), reads=[vecs[1]], writes=[hvec[1]])
        dve.op(lambda e: e.tensor_scalar(out=hvec[0][:, 32:48], in0=cdec[0], scalar1=0.5, scalar2=None, op0=ALU.mult), reads=[cdec[1]], writes=[hvec[1]])
        dve.op(lambda e: e.memset(halft[0], 0.5), writes=[halft[1]])
        ar.pop()

        if "W" in phases:
            ar.push()
            wst = [ar.f32(INW, f"wst{i}") for i in range(2)]
            wbf = [ar.bf16(INW, f"wbf{i}") for i in range(2)]
            for kc in range(16):
                a = wst[kc % 2]; b = wbf[kc % 2]
                sp.dma(lambda e, a=a, kc=kc: e.dma_start(out=a[0], in_=w_in[kc * 128:(kc + 1) * 128, :]), writes=[a[1]])
                q = dve if kc % 2 == 0 else pool
                q.op(lambda e, a=a, b=b, kc=kc: e.tensor_scalar(out=b[0], in0=a[0], scalar1=vecs[0][:, VG_MIX + kc:VG_MIX + kc + 1], scalar2=None, op0=ALU.mult),
                     reads=[a[1], vecs[1]], writes=[b[1]])
                sp.dma(lambda e, b=b, kc=kc: e.dma_start(out=winb[kc], in_=b[0]), reads=[b[1]])
            ar.pop()
            ebf_res = Res("ebf")
            for t, table in enumerate((edown, eup)):
                for ch in range(4):
                    rows = slice(ch * 4096, (ch + 1) * 4096)
                    pool.dma(lambda e, rows=rows, t=t, table=table: e.dma_start(out=ebf[rows, t * D:(t + 1) * D], in_=table[rows, :]))

        if "P0" in phases:
            ar.push()
            NXB = 3
            xb = [ar.f32(D, f"x{i}") for i in range(NXB)]
            xn = [ar.bf16(D, f"xn{i}") for i in range(2)]
            junk = ar.bf16(D, "junk")
            st = [ar.bf16(16 * BLK, f"hTs{i}") for i in range(2)]
            stat = [ar.f32(4, f"stat{i}") for i in range(4)]
            ntile = NV // 128

            def load(i):
                b = xb[i % NXB]
                sp.dma(lambda e, b=b, i=i: e.dma_start(out=b[0], in_=xv[i * 128:(i + 1) * 128, :]), writes=[b[1]])
            load(0)
            load(1)
            for i in range(ntile):
                if i + 2 < ntile:
                    load(i + 2)
                b = xb[i % NXB]; n = xn[i % 2]; s = stat[i % 4]; stg = st[(i // 4) % 2]
                act.op(lambda e, b=b, s=s: e.activation(out=junk[0], in_=b[0], func=AF.Square, accum_out=s[0][:, 0:1]),
                       reads=[b[1]], writes=[junk[1], s[1]])
                act.op(lambda e, s=s: e.activation(out=s[0][:, 1:2], in_=s[0][:, 0:1], func=AF.Sqrt, bias=cst[0][:, 0:1], scale=1.0 / D),
                       reads=[s[1], cst[1]], writes=[s[1]])
                dve.op(lambda e, s=s: e.reciprocal(out=s[0][:, 2:3], in_=s[0][:, 1:2]), reads=[s[1]], writes=[s[1]])
                dve.op(lambda e, b=b, n=n, s=s: e.tensor_scalar(out=n[0], in0=b[0], scalar1=s[0][:, 2:3], scalar2=None, op0=ALU.mult),
                       reads=[b[1], s[1]], writes=[n[1]])
                for half in range(2):
                    bank = ps[(i % 2) * 2 + half]
                    pv = bank[0].bitcast(BF16)
                    for j in range(8):
                        kc = half * 8 + j
                        pe.op(lambda e, pv=pv, n=n, j=j, kc=kc: e.transpose(out=pv[:, j * 128:(j + 1) * 128], in_=n[0][:, kc * 128:(kc + 1) * 128], identity=idb[0]),
                              reads=[n[1], idb[1]], writes=[bank[1]])
                    dst = stg[0].rearrange("p (k t) -> p k t", k=16)[:, half * 8:(half + 1) * 8, (i % 4) * 128:(i % 4 + 1) * 128]
                    src = pv.rearrange("p (k t) -> p k t", k=8)
                    if half == 0:
                        act.op(lambda e, dst=dst, src=src: e.copy(out=dst, in_=src), reads=[bank[1]], writes=[stg[1]])
                    else:
                        dve.op(lambda e, dst=dst, src=src: e.tensor_copy(out=dst, in_=src), reads=[bank[1]], writes=[stg[1]])
                if i % 4 == 3:
                    t0 = (i // 4) * BLK
                    sp.dma(lambda e, stg=stg, t0=t0: e.dma_start(out=hT[t0 // BLK], in_=stg[0]), reads=[stg[1]])
            ar.pop()
            fw.barrier()

        def load_w(dst, col0, ncol, q=None):
            (q or sp).dma(lambda e: e.dma_start(out=dst[0].rearrange("p (k c) -> p k c", k=16),
                                                in_=winb[:, :, col0:col0 + ncol].rearrange("k p c -> p k c")), writes=[dst[1]])

        def proj(bank, wt, ncol, c0, hb, n=BLK, cw=128):
            for kc in range(16):
                pe.op(lambda e, kc=kc: e.matmul(bank[0][0:cw, 0:n], lhsT=wt[0][:, kc * ncol + c0:kc * ncol + c0 + cw],
                                                rhs=hb[0][:, kc * BLK:kc * BLK + n], start=(kc == 0), stop=(kc == 15)),
                      reads=[wt[1], hb[1]], writes=[bank[1]])

        if "P1" in phases:
            for s in range(len(cfg.seq_lens)):
                LV, LO, V0, O0 = cfg.lv[s], cfg.lo[s], cfg.voff[s], cfg.ooff[s]
                NB = LV // BLK
                QB = min(BLK, LO)
                NQB = LO // QB
                for g in range(NKV):
                    ar.push()
                    KT = ar.bf16(LV, "KT")
                    Vt = ar.bf16(LV, "V")
                    QT = ar.bf16(4 * LO, "QT")
                    wq = ar.bf16(16 * 512, "wq")
                    wk = ar.bf16(16 * 128, "wk")
                    wv = ar.bf16(16 * 128, "wv")
                    km = ar.f32(LV // 128, "km")
                    hbs = [ar.bf16(16 * BLK, f"hb{i}") for i in range(2)]
                    cst_cs = [ar.f32(2 * BLK, f"cs{i}") for i in range(2)]
                    zb = [ar.bf16(BLK, f"zb{i}") for i in range(2)]
                    sq = [ar.bf16(BLK, f"sq{i}") for i in range(2)]
                    rstd = [ar.f32(BLK, f"rstd{i}") for i in range(2)]
                    t1 = [ar.f32(BLK, f"t1{i}") for i in range(2)]
                    t2 = [ar.f32(BLK, f"t2{i}") for i in range(2)]
                    vt = ar.bf16(BLK, "vt")
                    pt = [ar.bf16(BLK, f"pt{i}") for i in range(3)]
                    rs = [ar.f32(BLK, f"rs{i}") for i in range(2)]
                    ob = [ar.bf16(BLK, f"ob{i}") for i in range(2)]
                    pacc = [ar.f32(BLK, f"pacc{i}") for i in range(4)]
                    load_w(wq, g * 512, 512)
                    load_w(wk, 1024 + g * 128, 128)
                    load_w(wv, 1280 + g * 128, 128)
                    sp.dma(lambda e: e.dma_start(out=km[0], in_=kmask[:, V0 // 128:(V0 + LV) // 128]), writes=[km[1]])
                    cnt = [0]

                    def qknorm_rope(bank, gcol, rot, csb, dst):
                        i = cnt[0] % 2
                        cnt[0] += 1
                        z = bank[0]
                        act.op(lambda e: e.copy(out=zb[i][0], in_=z), reads=[bank[1]], writes=[zb[i][1]])
                        act.op(lambda e: e.activation(out=sq[i][0], in_=z, func=AF.Square), reads=[bank[1]], writes=[sq[i][1]])
                        pe.op(lambda e: e.matmul(ps[2][0], lhsT=onesb[0], rhs=sq[i][0], start=True, stop=True),
                              reads=[onesb[1], sq[i][1]], writes=[ps[2][1]])
                        pe.op(lambda e: e.matmul(ps[3][0], lhsT=rot[0], rhs=zb[i][0], start=True, stop=True),
                              reads=[rot[1], zb[i][1]], writes=[ps[3][1]])
                        act.op(lambda e: e.activation(out=rstd[i][0], in_=ps[2][0], func=AF.Sqrt, bias=cst[0][:, 0:1], scale=1.0 / DH),
                               reads=[ps[2][1], cst[1]], writes=[rstd[i][1]])
                        dve.op(lambda e: e.reciprocal(out=rstd[i][0], in_=rstd[i][0]), reads=[rstd[i][1]], writes=[rstd[i][1]])
                        dve.op(lambda e: e.scalar_tensor_tensor(out=t1[i][0], in0=z, scalar=gcol, in1=csb[0][:, 0:BLK], op0=ALU.mult, op1=ALU.mult),
                               reads=[bank[1], vecs[1], csb[1]], writes=[t1[i][1]])
                        dve.op(lambda e: e.tensor_tensor(out=t2[i][0], in0=ps[3][0], in1=csb[0][:, BLK:2 * BLK], op=ALU.mult),
                               reads=[ps[3][1], csb[1]], writes=[t2[i][1]])
                        pool.op(lambda e: e.tensor_tensor(out=t1[i][0], in0=t1[i][0], in1=t2[i][0], op=ALU.add),
                                reads=[t1[i][1], t2[i][1]], writes=[t1[i][1]])
                        pool.op(lambda e: e.tensor_tensor(out=dst, in0=t1[i][0], in1=rstd[i][0], op=ALU.mult),
                                reads=[t1[i][1], rstd[i][1]], writes=[dst_res[0]])

                    dst_res = [None]

                    def ldblk(b):
                        hb = hbs[b % 2]; c = cst_cs[b % 2]
                        t0 = V0 + b * BLK
                        sp.dma(lambda e: e.dma_start(out=hb[0], in_=hT[t0 // BLK]), writes=[hb[1]])
                        sp.dma(lambda e: e.dma_start(out=c[0].rearrange("p (a t) -> p a t", a=2),
                                                     in_=cs[:, :, t0:t0 + BLK].rearrange("a p t -> p a t")), writes=[c[1]])
                    ldblk(0)
                    zi = 0
                    for b in range(NB):
                        if b + 1 < NB:
                            ldblk(b + 1)
                        hb = hbs[b % 2]; c = cst_cs[b % 2]
                        bank = ps[zi % 2]; zi += 1
                        proj(bank, wk, 128, 0, hb)
                        dst_res[0] = KT[1]
                        qknorm_rope(bank, vecs[0][:, VG_K:VG_K + 1], rotk, c, KT[0][:, b * BLK:(b + 1) * BLK])
                        bank = ps[zi % 2]; zi += 1
                        proj(bank, wv, 128, 0, hb)
                        act.op(lambda e, bank=bank: e.copy(out=vt[0], in_=bank[0]), reads=[bank[1]], writes=[vt[1]])
                        pv = ps[4][0].bitcast(BF16)
                        for j in range(4):
                            pe.op(lambda e, j=j, pv=pv: e.transpose(out=pv[:, j * 128:(j + 1) * 128], in_=vt[0][:, j * 128:(j + 1) * 128], identity=idb[0]),
                                  reads=[vt[1], idb[1]], writes=[ps[4][1]])
                        dve.op(lambda e, b=b, pv=pv: e.tensor_copy(out=Vt[0][:, b * BLK:(b + 1) * BLK], in_=pv[:, 0:BLK]),
                               reads=[ps[4][1]], writes=[Vt[1]])
                        if b * BLK < LO:
                            for hh in range(4):
                                bank = ps[zi % 2]; zi += 1
                                proj(bank, wq, 512, hh * 128, hb, n=QB)
                                dst_res[0] = QT[1]
                                qknorm_rope(bank, vecs[0][:, VG_Q:VG_Q + 1], rotq, c, QT[0][:, hh * LO + b * BLK:hh * LO + b * BLK + QB])
                    it = 0
                    for hh in range(4):
                        for qb in range(NQB):
                            Ob = ps[(it % 2) * 2]; Sb = ps[(it % 2) * 2 + 1]
                            r = rs[it % 2]; o = ob[it % 2]
                            qsl = QT[0][:, hh * LO + qb * QB:hh * LO + (qb + 1) * QB]
                            nkc = LV // 128

                            def S_(kc):
                                Sk = ps[5 + kc % 3]
                                pe.op(lambda e: e.matmul(Sk[0][:, 0:QB], lhsT=KT[0][:, kc * 128:(kc + 1) * 128], rhs=qsl, start=True, stop=True),
                                      reads=[KT[1], QT[1]], writes=[Sk[1]])

                            def E_(kc):
                                Sk = ps[5 + kc % 3]; p = pt[kc % 3]
                                act.op(lambda e: e.activation(out=p[0][:, 0:QB], in_=Sk[0][:, 0:QB], func=AF.Exp, bias=km[0][:, kc:kc + 1], scale=DH ** -0.5),
                                       reads=[Sk[1], km[1]], writes=[p[1]])

                            def OV_(kc):
                                p = pt[kc % 3]
                                pe.op(lambda e: e.matmul(Ob[0][:, 0:QB], lhsT=Vt[0][:, kc * 128:(kc + 1) * 128], rhs=p[0][:, 0:QB], start=(kc == 0), stop=(kc == nkc - 1)),
                                      reads=[Vt[1], p[1]], writes=[Ob[1]])
                                pa = pacc[(it % 2) * 2 + kc % 2]
                                if kc < 2:
                                    dve.op(lambda e: e.tensor_copy(out=pa[0][:, 0:QB], in_=p[0][:, 0:QB]), reads=[p[1]], writes=[pa[1]])
                                else:
                                    dve.op(lambda e: e.tensor_tensor(out=pa[0][:, 0:QB], in0=pa[0][:, 0:QB], in1=p[0][:, 0:QB], op=ALU.add), reads=[p[1], pa[1]], writes=[pa[1]])

                            S_(0)
                            if nkc > 1:
                                S_(1)
                            for kc in range(nkc):
                                E_(kc)
                                if kc + 2 < nkc:
                                    S_(kc + 2)
                                OV_(kc)
                            pa0 = pacc[(it % 2) * 2]; pa1 = pacc[(it % 2) * 2 + 1]
                            if nkc > 1:
                                dve.op(lambda e, pa0=pa0, pa1=pa1: e.tensor_tensor(out=pa0[0][:, 0:QB], in0=pa0[0][:, 0:QB], in1=pa1[0][:, 0:QB], op=ALU.add), reads=[pa0[1], pa1[1]], writes=[pa0[1]])
                            pe.op(lambda e, pa0=pa0, Sb=Sb: e.matmul(Sb[0][:, 0:QB], lhsT=onesf[0], rhs=pa0[0][:, 0:QB], start=True, stop=True),
                                  reads=[onesf[1], pa0[1]], writes=[Sb[1]])
                            dve.op(lambda e, r=r, Sb=Sb: e.reciprocal(out=r[0][:, 0:QB], in_=Sb[0][:, 0:QB]), reads=[Sb[1]], writes=[r[1]])
                            dve.op(lambda e, r=r, o=o, Ob=Ob: e.tensor_tensor(out=o[0][:, 0:QB], in0=Ob[0][:, 0:QB], in1=r[0][:, 0:QB], op=ALU.mult),
                                   reads=[Ob[1], r[1]], writes=[o[1]])
                            head = g * 4 + hh
                            t0 = O0 + qb * QB
                            sp.dma(lambda e, o=o, head=head, t0=t0: e.dma_start(out=mixT[head, :, t0:t0 + QB], in_=o[0][:, 0:QB]), reads=[o[1]])
                            it += 1
                    ar.pop()
            fw.barrier()

        if "P2" in phases:
            for s in range(len(cfg.seq_lens)):
                LV, LO, V0, O0 = cfg.lv[s], cfg.lo[s], cfg.voff[s], cfg.ooff[s]
                NB = LV // BLK
                QB = min(BLK, LO)
                NOB = LO // QB
                assert QB == BLK or NOB == 1
                for c in range(8):
                    ar.push()
                    xc = ar.f32(LV, "xc")
                    gy = ar.f32(LO, "gy")
                    hown = [ar.f32(LO, f"hown{d}") for d in range(2)]
                    wxr = ar.bf16(16 * 128, "wxr")
                    wyr = ar.bf16(16 * 128, "wyr")
                    gst = ar.f32(128, "gst")
                    gm = [ar.bf16(128, f"gm{i}") for i in range(4)]
                    stt = [ar.f32(8, f"state{d}") for d in range(2)]
                    lo_t = [ar.bf16(BLK, f"lo{i}") for i in range(2)]
                    ar.push()
                    hbs = [ar.bf16(16 * BLK, f"hb{i}") for i in range(2)]
                    S = [ar.f32(BLK + 8, f"S{i}") for i in range(2)]
                    first = ar.f32(8, "first")
                    load_w(wxr, 1536 + c * 128, 128)
                    load_w(wyr, 2560 + c * 128, 128)
                    for gi in range(4):
                        sp.dma(lambda e, gi=gi: e.dma_start(out=gst[0], in_=wg_d[gi * 8 + c]), writes=[gst[1]])
                        dve.op(lambda e, gi=gi: e.tensor_copy(out=gm[gi][0], in_=gst[0]), reads=[gst[1]], writes=[gm[gi][1]])
                    cw = [vecs[0][:, VCONVW + tap * 8 + c:VCONVW + tap * 8 + c + 1] for tap in range(4)]
                    cb = vecs[0][:, VCONVB + c:VCONVB + c + 1]

                    def ldblk(j, b):
                        hb = hbs[j % 2]
                        t0 = V0 + b * BLK
                        sp.dma(lambda e: e.dma_start(out=hb[0], in_=hT[t0 // BLK]), writes=[hb[1]])

                    def conv(Sx, c0, n, dst):
                        dve.op(lambda e: e.tensor_scalar(out=dst, in0=Sx[0][:, c0 - 2:c0 - 2 + n], scalar1=cw[0], scalar2=cb, op0=ALU.mult, op1=ALU.add),
                               reads=[Sx[1], vecs[1]], writes=[xc[1]])
                        for tap in range(1, 4):
                            dve.op(lambda e, tap=tap: e.scalar_tensor_tensor(out=dst, in0=Sx[0][:, c0 - 2 + tap:c0 - 2 + tap + n], scalar=cw[tap], in1=dst, op0=ALU.mult, op1=ALU.add),
                                   reads=[Sx[1], vecs[1], xc[1]], writes=[xc[1]])

                    order = [NB - 1] + list(range(NB))
                    ldblk(0, order[0])
                    zi = 0
                    for j, b in enumerate(order):
                        if j + 1 < len(order):
                            ldblk(j + 1, order[j + 1])
                        hb = hbs[j % 2]; Sx = S[j % 2]; Sp = S[(j + 1) % 2]
                        bank = ps[zi % 2]; zi += 1
                        proj(bank, wxr, 128, 0, hb)
                        act.op(lambda e, Sx=Sx, bank=bank: e.copy(out=Sx[0][:, 3:3 + BLK], in_=bank[0]), reads=[bank[1]], writes=[Sx[1]])
                        if j > 0:
                            pool.op(lambda e, Sx=Sx, Sp=Sp: e.tensor_copy(out=Sx[0][:, 0:3], in_=Sp[0][:, BLK:BLK + 3]), reads=[Sp[1]], writes=[Sx[1]])
                            if b == 0:
                                conv(Sx, 3, BLK - 1, xc[0][:, 0:BLK - 1])
                                pool.op(lambda e, Sx=Sx: e.tensor_copy(out=first[0][:, 0:1], in_=Sx[0][:, 3:4]), reads=[Sx[1]], writes=[first[1]])
                            else:
                                conv(Sx, 2, BLK, xc[0][:, b * BLK - 1:(b + 1) * BLK - 1])
                            if b == NB - 1:
                                pool.op(lambda e, Sx=Sx: e.tensor_copy(out=Sx[0][:, BLK + 3:BLK + 4], in_=first[0][:, 0:1]), reads=[first[1]], writes=[Sx[1]])
                                conv(Sx, BLK + 2, 1, xc[0][:, LV - 1:LV])
                        if j > 0 and b * BLK < LO:
                            bank = ps[zi % 2]; zi += 1
                            proj(bank, wyr, 128, 0, hb, n=QB)
                            act.op(lambda e, b=b, bank=bank: e.activation(out=gy[0][:, b * BLK:b * BLK + QB], in_=bank[0][:, 0:QB], func=AF.Gelu),
                                   reads=[bank[1]], writes=[gy[1]])
                    ar.pop()
                    ar.push()
                    NSET = 4
                    xbb = [ar.bf16(BLK, f"xbb{i}") for i in range(NSET)]
                    rr = [ar.f32(BLK, f"rr{i}") for i in range(NSET)]
                    aa = [ar.f32(BLK, f"aa{i}") for i in range(NSET)]
                    ssq = [ar.f32(BLK, f"ssq{i}") for i in range(NSET)]
                    ii = [ar.f32(BLK, f"ii{i}") for i in range(NSET)]
                    uu = [ar.f32(BLK, f"uu{i}") for i in range(NSET)]
                    mk = [ar.f32(BLK, f"mk{i}") for i in range(NSET)]
                    hh_ = [ar.f32(BLK, f"hh{i}") for i in range(NSET)]
                    nob = LO // BLK
                    border = [list(range(nob, NB)) + list(range(nob)), list(range(NB - 1, -1, -1))]
                    for d in range(2):
                        dve.op(lambda e, d=d: e.memset(stt[d][0][:, 0:1], 0.0), writes=[stt[d][1]])
                    for j in range(NB):
                        bl = [border[0][j], border[1][j]]
                        st_ = [0 * 2 + j % 2, 1 * 2 + j % 2]
                        xs = [xc[0][:, bl[d] * BLK:(bl[d] + 1) * BLK] for d in range(2)]
                        for d in range(2):
                            i = st_[d]; t0 = V0 + bl[d] * BLK
                            sp.dma(lambda e, i=i, t0=t0, d=d: e.dma_start(out=mk[i][0], in_=lmask[d:d + 1, t0:t0 + BLK].to_broadcast([128, BLK])), writes=[mk[i][1]])
                        for d in range(2):
                            i = st_[d]
                            pool.op(lambda e, i=i, d=d: e.tensor_copy(out=xbb[i][0], in_=xs[d]), reads=[xc[1]], writes=[xbb[i][1]])
                        for d in range(2):
                            i = st_[d]; ba = ps[2 + d * 2]; bx = ps[3 + d * 2]
                            pe.op(lambda e, i=i, d=d, ba=ba: e.matmul(ba[0], lhsT=gm[0 * 2 + d][0], rhs=xbb[i][0], start=True, stop=True),
                                  reads=[gm[d][1], xbb[i][1]], writes=[ba[1]])
                            pe.op(lambda e, i=i, d=d, bx=bx: e.matmul(bx[0], lhsT=gm[1 * 2 + d][0], rhs=xbb[i][0], start=True, stop=True),
                                  reads=[gm[2 + d][1], xbb[i][1]], writes=[bx[1]])
                        for d in range(2):
                            i = st_[d]; ba = ps[2 + d * 2]
                            act.op(lambda e, i=i, d=d, ba=ba: e.activation(out=rr[i][0], in_=ba[0], func=AF.Tanh, bias=hvec[0][:, d * 8 + c:d * 8 + c + 1], scale=0.5),
                                   reads=[ba[1], hvec[1]], writes=[rr[i][1]])
                        for d in range(2):
                            i = st_[d]; bx = ps[3 + d * 2]
                            act.op(lambda e, i=i, d=d, bx=bx: e.activation(out=ii[i][0], in_=bx[0], func=AF.Tanh, bias=hvec[0][:, 16 + d * 8 + c:16 + d * 8 + c + 1], scale=0.5),
                                   reads=[bx[1], hvec[1]], writes=[ii[i][1]])
                        for d in range(2):
                            i = st_[d]
                            act.op(lambda e, i=i, d=d: e.activation(out=aa[i][0], in_=rr[i][0], func=AF.Exp, bias=hvec[0][:, 32 + d * 8 + c:32 + d * 8 + c + 1], scale=hvec[0][:, 32 + d * 8 + c:32 + d * 8 + c + 1]),
                                   reads=[rr[i][1], hvec[1]], writes=[aa[i][1]])
                        for d in range(2):
                            i = st_[d]
                            act.op(lambda e, i=i: e.activation(out=ssq[i][0], in_=aa[i][0], func=AF.Square), reads=[aa[i][1]], writes=[ssq[i][1]])
                        for d in range(2):
                            i = st_[d]
                            dve.op(lambda e, i=i, d=d: e.scalar_tensor_tensor(out=uu[i][0], in0=ii[i][0], scalar=1.0, in1=xs[d], op0=ALU.add, op1=ALU.mult),
                                   reads=[ii[i][1], xc[1]], writes=[uu[i][1]])
                        for d in range(2):
                            i = st_[d]
                            act.op(lambda e, i=i: e.activation(out=ssq[i][0], in_=ssq[i][0], func=AF.Sqrt, bias=cst[0][:, 2:3], scale=-0.25),
                                   reads=[ssq[i][1], cst[1]], writes=[ssq[i][1]])
                        for d in range(2):
                            i = st_[d]
                            pool.op(lambda e, i=i: e.tensor_tensor(out=aa[i][0], in0=aa[i][0], in1=mk[i][0], op=ALU.mult), reads=[aa[i][1], mk[i][1]], writes=[aa[i][1]])
                        for d in range(2):
                            i = st_[d]
                            dve.op(lambda e, i=i: e.tensor_tensor(out=uu[i][0], in0=uu[i][0], in1=ssq[i][0], op=ALU.mult), reads=[uu[i][1], ssq[i][1]], writes=[uu[i][1]])
                        for d in range(2):
                            i = st_[d]; b_ = bl[d]
                            if b_ * BLK < LO:
                                hdst = hown[d][0][:, b_ * BLK:(b_ + 1) * BLK]; hres = hown[d][1]
                            else:
                                hdst = hh_[i][0]; hres = hh_[i][1]
                            if d == 0:
                                dve.op(lambda e, i=i, hdst=hdst: e.tensor_tensor_scan(out=hdst, data0=aa[i][0], data1=uu[i][0], initial=stt[0][0][:, 0:1], op0=ALU.mult, op1=ALU.add),
                                       reads=[aa[i][1], uu[i][1], stt[0][1]], writes=[hres])
                                dve.op(lambda e, hdst=hdst: e.tensor_copy(out=stt[0][0][:, 0:1], in_=hdst[:, BLK - 1:BLK]), reads=[hres], writes=[stt[0][1]])
                            else:
                                dve.op(lambda e, i=i, hdst=hdst: e.tensor_tensor_scan(out=hdst[:, ::-1], data0=aa[i][0][:, ::-1], data1=uu[i][0][:, ::-1], initial=stt[1][0][:, 0:1], op0=ALU.mult, op1=ALU.add),
                                       reads=[aa[i][1], uu[i][1], stt[1][1]], writes=[hres])
                                dve.op(lambda e, hdst=hdst: e.tensor_copy(out=stt[1][0][:, 0:1], in_=hdst[:, 0:1]), reads=[hres], writes=[stt[1][1]])
                    for ob_ in range(NOB):
                        lt = lo_t[ob_ % 2]
                        sl = slice(ob_ * QB, (ob_ + 1) * QB)
                        pool.op(lambda e, sl=sl: e.tensor_tensor(out=hown[0][0][:, sl], in0=hown[0][0][:, sl], in1=hown[1][0][:, sl], op=ALU.add),
                                reads=[hown[0][1], hown[1][1]], writes=[hown[0][1]])
                        pool.op(lambda e, sl=sl, lt=lt: e.tensor_tensor(out=lt[0][:, 0:QB], in0=hown[0][0][:, sl], in1=gy[0][:, sl], op=ALU.mult),
                                reads=[hown[0][1], gy[1]], writes=[lt[1]])
                        t0 = O0 + ob_ * QB
                        sp.dma(lambda e, lt=lt, t0=t0: e.dma_start(out=mixT[8 + c, :, t0:t0 + QB], in_=lt[0][:, 0:QB]), reads=[lt[1]])
                    ar.pop()
                    ar.pop()
            fw.barrier()

        if "P3a" in phases:
            ar.push()
            wo = ar.bf16(16 * D, "wo")
            wst = [ar.f32(D, f"wost{i}") for i in range(2)]
            for f in range(16):
                a = wst[f % 2]
                row0 = f * 128 if f < 8 else 1024 + (f - 8) * 128
                gcol = vecs[0][:, VGA + f:VGA + f + 1] if f < 8 else vecs[0][:, VGL + f - 8:VGL + f - 7]
                sp.dma(lambda e, a=a, row0=row0: e.dma_start(out=a[0], in_=w_out[row0:row0 + 128, :]), writes=[a[1]])
                q = dve if f % 2 == 0 else pool
                q.op(lambda e, a=a, f=f, gcol=gcol: e.tensor_scalar(out=wo[0][:, f * D:(f + 1) * D], in0=a[0], scalar1=gcol, scalar2=None, op0=ALU.mult),
                     reads=[a[1], vecs[1]], writes=[wo[1]])
            MB = 256 if NOWN % 512 else 512
            mts = [ar.bf16(16 * MB, f"mt{i}") for i in range(2)]
            sqs = [ar.bf16(16 * MB, f"msq{i}") for i in range(2)]
            xo = [ar.f32(D, f"xo{i}") for i in range(2)]
            x2 = [ar.f32(D, f"x2{i}") for i in range(2)]
            stat = [ar.f32(4, f"st{i}") for i in range(2)]
            own_rows = []
            for s in range(len(cfg.seq_lens)):
                for t in range(0, cfg.lo[s], 128):
                    own_rows.append(cfg.voff[s] + t)
            ntile = NOWN // 128
            tpb = MB // 128
            for i in range(ntile):
                mt = mts[(i // tpb) % 2]; msq = sqs[(i // tpb) % 2]
                if i % tpb == 0:
                    t0 = i * 128
                    sp.dma(lambda e, mt=mt, t0=t0: e.dma_start(out=mt[0].rearrange("p (k t) -> p k t", k=16),
                                                                 in_=mixT[:, :, t0:t0 + MB].rearrange("k p t -> p k t")), writes=[mt[1]])
                    pool.op(lambda e, mt=mt, msq=msq: e.tensor_tensor(out=msq[0], in0=mt[0], in1=mt[0], op=ALU.mult), reads=[mt[1]], writes=[msq[1]])
                xi = xo[i % 2]; xr2 = x2[i % 2]; stt = stat[i % 2]
                r0 = own_rows[i]
                sp.dma(lambda e, xi=xi, r0=r0: e.dma_start(out=xi[0], in_=xv[r0:r0 + 128, :]), writes=[xi[1]])
                tsl = (i % tpb) * 128
                for part in range(2):
                    for f8 in range(8):
                        f = part * 8 + f8
                        pe.op(lambda e, f=f, f8=f8, part=part, msq=msq, tsl=tsl: e.matmul(ps[6 + part][0][:, 0:1], lhsT=msq[0][:, f * MB + tsl:f * MB + tsl + 128], rhs=onesb[0][:, 0:1], start=(f8 == 0), stop=(f8 == 7)),
                              reads=[msq[1], onesb[1]], writes=[ps[6 + part][1]])
                    act.op(lambda e, part=part, stt=stt: e.activation(out=stt[0][:, part:part + 1], in_=ps[6 + part][0][:, 0:1], func=AF.Sqrt, bias=cst[0][:, 0:1], scale=1.0 / 1024.0),
                           reads=[ps[6 + part][1], cst[1]], writes=[stt[1]])
                dve.op(lambda e, stt=stt: e.reciprocal(out=stt[0][:, 2:4], in_=stt[0][:, 0:2]), reads=[stt[1]], writes=[stt[1]])
                for qd in range(4):
                    ba = ps[(qd % 2) * 2]; bl = ps[(qd % 2) * 2 + 1]
                    for part, bank in ((0, ba), (1, bl)):
                        for f8 in range(8):
                            f = part * 8 + f8
                            pe.op(lambda e, f=f, f8=f8, bank=bank, mt=mt, tsl=tsl, qd=qd: e.matmul(bank[0], lhsT=mt[0][:, f * MB + tsl:f * MB + tsl + 128], rhs=wo[0][:, f * D + qd * 512:f * D + (qd + 1) * 512], start=(f8 == 0), stop=(f8 == 7)),
                                  reads=[mt[1], wo[1]], writes=[bank[1]])
                    csl = slice(qd * 512, (qd + 1) * 512)
                    dve.op(lambda e, ba=ba, csl=csl, stt=stt, xi=xi, xr2=xr2: e.scalar_tensor_tensor(out=xr2[0][:, csl], in0=ba[0], scalar=stt[0][:, 2:3], in1=xi[0][:, csl], op0=ALU.mult, op1=ALU.add),
                           reads=[ba[1], stt[1], xi[1]], writes=[xr2[1]])
                    dve.op(lambda e, bl=bl, csl=csl, stt=stt, xr2=xr2: e.scalar_tensor_tensor(out=xr2[0][:, csl], in0=bl[0], scalar=stt[0][:, 3:4], in1=xr2[0][:, csl], op0=ALU.mult, op1=ALU.add),
                           reads=[bl[1], stt[1], xr2[1]], writes=[xr2[1]])
                sp.dma(lambda e, xr2=xr2, i=i: e.dma_start(out=x2d[i * 128:(i + 1) * 128, :], in_=xr2[0]), reads=[xr2[1]])
            ar.pop()
            fw.barrier()

        if "P3b" in phases:
            ar.push()
            wqy = ar.bf16(16 * 1024, "wqy")
            rkb = ar.bf16(8 * 256, "rkb")
            gf = ar.f32(D, "gffn")
            ar.push()
            wst = [ar.f32(1024, f"wqst{i}") for i in range(2)]
            for kc in range(16):
                a_ = wst[kc % 2]
                sp.dma(lambda e, a_=a_, kc=kc: e.dma_start(out=a_[0], in_=w_query[kc * 128:(kc + 1) * 128, :]), writes=[a_[1]])
                q = dve if kc % 2 == 0 else pool
                q.op(lambda e, a_=a_, kc=kc: e.tensor_copy(out=wqy[0][:, kc * 1024:(kc + 1) * 1024], in_=a_[0]), reads=[a_[1]], writes=[wqy[1]])
            for h in range(8):
                a_ = wst[h % 2]
                sp.dma(lambda e, a_=a_, h=h: e.dma_start(out=a_[0][:, 0:256], in_=rk_d[h]), writes=[a_[1]])
                dve.op(lambda e, a_=a_, h=h: e.tensor_copy(out=rkb[0][:, h * 256:(h + 1) * 256], in_=a_[0][:, 0:256]), reads=[a_[1]], writes=[rkb[1]])
            sp.dma(lambda e: e.dma_start(out=gf[0], in_=gffn_d.to_broadcast([128, D])), writes=[gf[1]])
            ar.pop()
            jA = ar.bf16(D, "pjA")
            jD = [ar.bf16(D, f"pjD{i}") for i in range(2)]
            tslot = [Res(f"tslot{i}") for i in range(8)]
            xnT = ar.bf16(16 * 128, "pxnT")
            qT = ar.bf16(8 * 128, "pqT")
            sc = ar.f32(2048, "psc")
            tmp = ar.f32(2048, "ptmp")
            cid = ar.f32(2048, "pcid")
            vals = ar.f32(256, "pvals")
            idxu = ar.u32(256, "pidx")
            idxf = ar.f32(256, "pidxf")
            best = ar.f32(128, "pbest")
            gsum = ar.f32(8, "pgsum")
            negm = ar.f32(8, "pnegm")
            eidf = ar.f32(128, "peidf")
            stat = ar.f32(4, "pstat")
            x2 = [ar.f32(D, f"px2{i}") for i in range(2)]
            xnb = [ar.bf16(D, f"pxnb{i}") for i in range(2)]
            eidu = [ar.u32(128, f"peidu{i}") for i in range(2)]
            gate = [ar.f32(128, f"pgate{i}") for i in range(2)]
            hid = [ar.f32(128, f"phid{i}") for i in range(2)]
            gel = [ar.f32(128, f"pgel{i}") for i in range(2)]
            wgt = [ar.f32(128, f"pwgt{i}") for i in range(2)]
            NDG = 8
            diag = [ar.bf16(128, f"pdiag{i}") for i in range(NDG)]
            NU = 9
            U = [ar.bf16(2 * D, f"U{i}") for i in range(NU)]
            ntile = NOWN // 128
            ebf_rows = ebf

            def prep(i):
                par = i % 2
                xt = x2[par]; xb_ = xnb[par]
                sp.dma(lambda e: e.dma_start(out=xt[0], in_=x2d[i * 128:(i + 1) * 128, :]), writes=[xt[1]])
                act.op(lambda e: e.activation(out=jA[0], in_=xt[0], func=AF.Square, accum_out=stat[0][:, 0:1]), reads=[xt[1]], writes=[jA[1], stat[1]])
                act.op(lambda e: e.activation(out=stat[0][:, 1:2], in_=stat[0][:, 0:1], func=AF.Sqrt, bias=cst[0][:, 0:1], scale=1.0 / D), reads=[stat[1], cst[1]], writes=[stat[1]])
                dve.op(lambda e: e.reciprocal(out=stat[0][:, 2:3], in_=stat[0][:, 1:2]), reads=[stat[1]], writes=[stat[1]])
                dve.op(lambda e: e.scalar_tensor_tensor(out=xb_[0], in0=xt[0], scalar=stat[0][:, 2:3], in1=gf[0], op0=ALU.mult, op1=ALU.mult),
                       reads=[xt[1], stat[1], gf[1]], writes=[xb_[1]])
                yield
                for half in range(2):
                    bank = ps[half]
                    pv = bank[0].bitcast(BF16)
                    for j in range(8):
                        kc = half * 8 + j
                        pe.op(lambda e, pv=pv, j=j, kc=kc: e.transpose(out=pv[:, j * 128:(j + 1) * 128], in_=xb_[0][:, kc * 128:(kc + 1) * 128], identity=idb[0]),
                              reads=[xb_[1], idb[1]], writes=[bank[1]])
                    act.op(lambda e, pv=pv, half=half: e.copy(out=xnT[0][:, half * 1024:(half + 1) * 1024], in_=pv), reads=[bank[1]], writes=[xnT[1]])
                for hq in range(2):
                    bank = ps[2 + hq]
                    for h4 in range(4):
                        h = hq * 4 + h4
                        for kc in range(16):
                            pe.op(lambda e, bank=bank, h=h, h4=h4, kc=kc: e.matmul(bank[0][:, h4 * 128:(h4 + 1) * 128], lhsT=wqy[0][:, kc * 1024 + h * 128:kc * 1024 + (h + 1) * 128],
                                                                                     rhs=xnT[0][:, kc * 128:(kc + 1) * 128], start=(kc == 0), stop=(kc == 15)),
                                  reads=[wqy[1], xnT[1]], writes=[bank[1]])
                    act.op(lambda e, bank=bank, hq=hq: e.copy(out=qT[0][:, hq * 512:(hq + 1) * 512], in_=bank[0]), reads=[bank[1]], writes=[qT[1]])
                yield
                for h in range(8):
                    bank = ps[h // 2]
                    pe.op(lambda e, bank=bank, h=h: e.matmul(bank[0][:, (h % 2) * 256:(h % 2 + 1) * 256], lhsT=qT[0][:, h * 128:(h + 1) * 128], rhs=rkb[0][:, h * 256:(h + 1) * 256], start=True, stop=True),
                          reads=[qT[1], rkb[1]], writes=[bank[1]])
                for bq in range(4):
                    act.op(lambda e, bq=bq: e.copy(out=sc[0][:, bq * 512:(bq + 1) * 512], in_=ps[bq][0]), reads=[ps[bq][1]], writes=[sc[1]])
                yield
                sgs = [sc[0][:, gq * 128:(gq + 1) * 128] for gq in range(16)]
                tgs = [tmp[0][:, gq * 128:(gq + 1) * 128] for gq in range(16)]
                for gq in range(16):
                    dve.op(lambda e, gq=gq: e.max(out=vals[0][:, gq * 16:gq * 16 + 8], in_=sgs[gq]), reads=[sc[1]],
                           writes=([vals[1]] if gq == 0 else []), ww=([] if gq == 0 else [vals[1]]))
                for gq in range(16):
                    dve.op(lambda e, gq=gq: e.match_replace(out=tgs[gq], in_to_replace=vals[0][:, gq * 16:gq * 16 + 8], in_values=sgs[gq], imm_value=-1e30),
                           reads=[sc[1], vals[1]], writes=([tmp[1]] if gq == 0 else []), ww=([] if gq == 0 else [tmp[1]]))
                yield
                for gq in range(16):
                    dve.op(lambda e, gq=gq: e.max(out=vals[0][:, gq * 16 + 8:gq * 16 + 16], in_=tgs[gq]), reads=[tmp[1]], ww=[vals[1]])
                for gq in range(16):
                    dve.op(lambda e, gq=gq: e.max_index(out=idxu[0][:, gq * 16:gq * 16 + 8], in_max=vals[0][:, gq * 16:gq * 16 + 8], in_values=sgs[gq]), reads=[sc[1], vals[1]],
                           writes=([idxu[1]] if gq == 0 else []), ww=([] if gq == 0 else [idxu[1]]))
                yield
                for gq in range(16):
                    dve.op(lambda e, gq=gq: e.max_index(out=idxu[0][:, gq * 16 + 8:gq * 16 + 16], in_max=vals[0][:, gq * 16 + 8:gq * 16 + 16], in_values=sgs[gq]), reads=[sc[1], vals[1]], ww=[idxu[1]])
                dve.op(lambda e: e.tensor_copy(out=idxf[0], in_=idxu[0]), reads=[idxu[1]], writes=[idxf[1]])
                v4 = vals[0].rearrange("p (h c k) -> p h c k", h=8, c=2)
                i4 = idxf[0].rearrange("p (h c k) -> p h c k", h=8, c=2)
                c4 = sc[0].rearrange("p (h a b) -> p h a b", h=8, a=16)
                d4 = cid[0].rearrange("p (h a b) -> p h a b", h=8, a=16)
                dve.op(lambda e: e.tensor_scalar(out=i4[:, :, 0, :], in0=i4[:, :, 0, :], scalar1=128.0, scalar2=None, op0=ALU.mult), reads=[idxf[1]], writes=[idxf[1]])
                dve.op(lambda e: e.tensor_tensor(out=c4, in0=v4[:, :, 0, :].unsqueeze(3).to_broadcast([128, 8, 16, 16]),
                                                 in1=v4[:, :, 1, :].unsqueeze(2).to_broadcast([128, 8, 16, 16]), op=ALU.add),
                       reads=[vals[1]], writes=[sc[1]])
                dve.op(lambda e: e.tensor_tensor(out=d4, in0=i4[:, :, 0, :].unsqueeze(3).to_broadcast([128, 8, 16, 16]),
                                                 in1=i4[:, :, 1, :].unsqueeze(2).to_broadcast([128, 8, 16, 16]), op=ALU.add),
                       reads=[idxf[1]], writes=[cid[1]])
                yield
                cgs = [sc[0][:, h * 256:(h + 1) * 256] for h in range(8)]
                ths = [tmp[0][:, h * 256:(h + 1) * 256] for h in range(8)]
                for h in range(8):
                    dve.op(lambda e, h=h: e.max(out=best[0][:, h * 16:h * 16 + 8], in_=cgs[h]), reads=[sc[1]],
                           writes=([best[1]] if h == 0 else []), ww=([] if h == 0 else [best[1]]))
                for h in range(8):
                    dve.op(lambda e, h=h: e.match_replace(out=ths[h], in_to_replace=best[0][:, h * 16:h * 16 + 8], in_values=cgs[h], imm_value=-1e30),
                           reads=[sc[1], best[1]], writes=([tmp[1]] if h == 0 else []), ww=([] if h == 0 else [tmp[1]]))
                for h in range(8):
                    dve.op(lambda e, h=h: e.max(out=best[0][:, h * 16 + 8:h * 16 + 16], in_=ths[h]), reads=[tmp[1]], ww=[best[1]])
                yield
                gt = gate[par]
                dve.op(lambda e: e.tensor_scalar(out=negm[0], in0=best[0].rearrange("p (h k) -> p h k", h=8)[:, :, 0], scalar1=-1.0, scalar2=None, op0=ALU.mult),
                       reads=[best[1]], writes=[negm[1]])
                for h in range(8):
                    act.op(lambda e, h=h: e.activation(out=gt[0][:, h * 16:(h + 1) * 16], in_=best[0][:, h * 16:(h + 1) * 16], func=AF.Exp, bias=negm[0][:, h:h + 1], accum_out=gsum[0][:, h:h + 1]),
                           reads=[best[1], negm[1]], writes=([gt[1], gsum[1]] if h == 0 else []), ww=([] if h == 0 else [gt[1], gsum[1]]))
                dve.op(lambda e: e.reciprocal(out=gsum[0], in_=gsum[0]), reads=[gsum[1]], writes=[gsum[1]])
                dve.op(lambda e: e.tensor_tensor(out=gt[0].rearrange("p (h k) -> p h k", h=8), in0=gt[0].rearrange("p (h k) -> p h k", h=8),
                                                 in1=gsum[0].unsqueeze(2).to_broadcast([128, 8, 16]), op=ALU.mult), reads=[gt[1], gsum[1]], writes=[gt[1]])
                for hk in range(128):
                    h = hk // 16
                    first = (hk == 0)
                    dve.op(lambda e, h=h, hk=hk: e.scalar_tensor_tensor(out=tmp[0][:, (hk % 8) * 256:(hk % 8 + 1) * 256], in0=sc[0][:, h * 256:(h + 1) * 256], scalar=best[0][:, hk:hk + 1],
                                                                      in1=cid[0][:, h * 256:(h + 1) * 256], op0=ALU.is_equal, op1=ALU.mult, accum_out=eidf[0][:, hk:hk + 1]),
                           reads=[sc[1], best[1], cid[1]], writes=([tmp[1], eidf[1]] if first else []) + [tslot[hk % 8]], ww=([] if first else [tmp[1], eidf[1]]))
                    if hk % 32 == 31:
                        yield
                dve.op(lambda e: e.tensor_scalar(out=eidf[0], in0=eidf[0], scalar1=float(NEXP - 1), scalar2=0.0, op0=ALU.min, op1=ALU.max), reads=[eidf[1]], writes=[eidf[1]])
                dve.op(lambda e: e.tensor_copy(out=eidu[par][0], in_=eidf[0]), reads=[eidf[1]], writes=[eidu[par][1]])

            ucount = [0]
            dcount = [0]
            jcount = [0]

            def main(i, gen):
                par = i % 2
                xt = x2[par]; xb_ = xnb[par]; hd = hid[par]; ge = gel[par]; wg_ = wgt[par]; gt = gate[par]
                for g4 in range(32):
                    us = []
                    for k4 in range(4):
                        hk = g4 * 4 + k4
                        u = U[ucount[0] % NU]
                        ucount[0] += 1
                        pool.dma(lambda e, u=u, hk=hk: e.indirect_dma_start(out=u[0], out_offset=None, in_=ebf_rows,
                                                                            in_offset=bass.IndirectOffsetOnAxis(ap=eidu[par][0][:, hk:hk + 1], axis=0)),
                                 reads=[eidu[par][1]], writes=[u[1]])
                        us.append(u)
                    for k4 in range(4):
                        hk = g4 * 4 + k4
                        u = us[k4]
                        strong = (hk == 0)
                        jd = jD[jcount[0] % 2]
                        jcount[0] += 1
                        dve.op(lambda e, u=u, hk=hk, jd=jd: e.scalar_tensor_tensor(out=jd[0], in0=u[0][:, 0:D], scalar=1.0, in1=xb_[0], op0=ALU.mult, op1=ALU.mult, accum_out=hd[0][:, hk:hk + 1]),
                               reads=[u[1], xb_[1]], writes=[jd[1]] + ([hd[1]] if strong else []), ww=([] if strong else [hd[1]]))
                    sl = slice(g4 * 4, g4 * 4 + 4)
                    act.op(lambda e, sl=sl: e.activation(out=ge[0][:, sl], in_=hd[0][:, sl], func=AF.Gelu), reads=[hd[1]],
                           writes=([ge[1]] if g4 == 0 else []), ww=([] if g4 == 0 else [ge[1]]))
                    for k4 in range(4):
                        hk = g4 * 4 + k4
                        u = us[k4]
                        dg = diag[dcount[0] % NDG]
                        dcount[0] += 1
                        dve.op(lambda e, dg=dg, hk=hk: e.tensor_scalar(out=dg[0], in0=idb[0], scalar1=ge[0][:, hk:hk + 1], scalar2=gt[0][:, hk:hk + 1], op0=ALU.mult, op1=ALU.mult),
                               reads=[idb[1], ge[1], gt[1]], writes=[dg[1]])
                        for q4 in range(4):
                            pe.op(lambda e, dg=dg, u=u, q4=q4, hk=hk: e.matmul(ps[4 + q4][0], lhsT=dg[0], rhs=u[0][:, D + q4 * 512:D + (q4 + 1) * 512], start=(hk == 0), stop=(hk == 127)),
                                  reads=[dg[1], u[1]], writes=[ps[4 + q4][1]])
                    if gen is not None and g4 % 2 == 1:
                        next(gen, None)
                if gen is not None:
                    for _ in gen:
                        pass
                for q4 in range(4):
                    csl = slice(q4 * 512, (q4 + 1) * 512)
                    dve.op(lambda e, q4=q4, csl=csl: e.tensor_tensor(out=xt[0][:, csl], in0=ps[4 + q4][0], in1=xt[0][:, csl], op=ALU.add),
                           reads=[ps[4 + q4][1], xt[1]], writes=[xt[1]])
                sp.dma(lambda e: e.dma_start(out=y[i * 128:(i + 1) * 128, :], in_=xt[0]), reads=[xt[1]])

            for _ in prep(0):
                pass
            for i in range(ntile):
                gen = prep(i + 1) if i + 1 < ntile else None
                main(i, gen)
            ar.pop()

        fw.emit(stack)
    return nc


def _rope_tables(L):
    half = DH // 2
    inv = (10000.0 ** (-np.arange(0, half, 2, dtype=np.float32) / half)).astype(np.float32)
    t = np.arange(L)
    row = (t // 64).astype(np.float32)
    col = (t % 64).astype(np.float32)
    ang_r = row[None, :] * inv[:, None]
    ang_c = col[None, :] * inv[:, None]
    ang = np.concatenate([ang_r, ang_r, ang_c, ang_c], axis=0).astype(np.float32)
    return np.cos(ang).astype(np.float32), np.sin(ang).astype(np.float32)


def _prot():
    P = np.zeros((128, 128), np.float32)
    for base in (0, 64):
        for j in range(32):
            P[base + j, base + j + 32] = -1.0
            P[base + j + 32, base + j] = 1.0
    return np.ascontiguousarray(P.T)


def prep_inputs(cfg, inp):
    f = np.float32
    seqs = [np.asarray(inp["x_prompt"], f)[0]] + [np.asarray(inp["x_sample"], f)[b] for b in range(np.asarray(inp["x_sample"]).shape[0])]
    seqs = seqs[:len(cfg.seq_lens)]
    vec = np.zeros((128, NVEC), f)
    vec[:, VG_MIX:VG_MIX + 16] = np.asarray(inp["g_mix"], f)[0].reshape(16, 128).T
    vec[:, VG_Q] = np.asarray(inp["g_q"], f)[0]
    vec[:, VG_K] = np.asarray(inp["g_k"], f)[0]
    vec[:, VCONVW:VCONVW + 32] = np.asarray(inp["conv_w"], f)[0].reshape(4, 8, 128).transpose(2, 0, 1).reshape(128, 32)
    vec[:, VCONVB:VCONVB + 8] = np.asarray(inp["conv_b"], f)[0].reshape(8, 128).T
    vec[:, VBA:VBA + 16] = np.asarray(inp["b_gate_a"], f)[0].reshape(2, 8, 128).transpose(2, 0, 1).reshape(128, 16)
    vec[:, VBX:VBX + 16] = np.asarray(inp["b_gate_x"], f)[0].reshape(2, 8, 128).transpose(2, 0, 1).reshape(128, 16)
    vec[:, VLAM:VLAM + 16] = np.asarray(inp["lru_lambda"], f)[0].reshape(2, 8, 128).transpose(2, 0, 1).reshape(128, 16)
    vec[:, VGA:VGA + 8] = np.asarray(inp["g_attn_out"], f)[0].reshape(8, 128).T
    vec[:, VGL:VGL + 8] = np.asarray(inp["g_lru_out"], f)[0].reshape(8, 128).T
    wg = np.zeros((32, 128, 128), f)
    for gi, name in enumerate(("w_gate_a", "w_gate_x")):
        w = np.asarray(inp[name], f)[0]
        for d in range(2):
            for c in range(8):
                m = wg[(gi * 2 + d) * 8 + c]
                m[0:64, 0:64] = w[d, 2 * c]
                m[64:128, 64:128] = w[d, 2 * c + 1]
    sk = np.asarray(inp["sub_keys"], f)[0]
    rk = np.zeros((8, 128, 256), f)
    for h in range(8):
        rk[h, 0:64, 0:128] = sk[h, 0].T
        rk[h, 64:128, 128:256] = sk[h, 1].T
    shared = {
        "vecs": vec, "w_in": np.ascontiguousarray(np.asarray(inp["w_in"], f)[0]),
        "w_out": np.ascontiguousarray(np.asarray(inp["w_out"], f)[0]),
        "w_query": np.ascontiguousarray(np.asarray(inp["w_query"], f)[0]),
        "wg": wg, "rk": rk, "gffn": np.ascontiguousarray(np.asarray(inp["g_ffn"], f)[0].reshape(1, D)),
        "ident": np.eye(128, dtype=f), "prot": _prot(),
        "edown": np.ascontiguousarray(np.asarray(inp["expert_down"], f)[0]),
        "eup": np.ascontiguousarray(np.asarray(inp["expert_up"], f)[0]),
    }
    tabs = [_rope_tables(L) for L in cfg.seq_lens]
    maps = []
    for c in range(cfg.ncores):
        xv = np.zeros((cfg.nv, D), f)
        csv = np.zeros((2, 128, cfg.nv), f)
        csv[0] = 1.0
        kmv = np.zeros((cfg.nv,), f)
        lm = np.ones((2, cfg.nv), f)
        for s, L in enumerate(cfg.seq_lens):
            lo = cfg.lo[s]; o0 = c * lo; v0 = cfg.voff[s]
            n_after = L - o0 - lo
            idx = np.concatenate([np.arange(o0, L), np.arange(0, o0)])
            pos = np.concatenate([np.arange(0, L - o0), np.arange(L - o0 + PAD, L + PAD)])
            xv[v0 + pos] = seqs[s][idx]
            csv[0][:, v0 + pos] = tabs[s][0][:, idx]
            csv[1][:, v0 + pos] = tabs[s][1][:, idx]
            kmv[v0 + L - o0:v0 + L - o0 + PAD] = -30000.0
            p0 = (L - o0 + PAD) if o0 > 0 else 0
            lm[0, v0 + p0] = 0.0
            lm[1, v0 + (L - o0 - 1)] = 0.0
        m = dict(shared)
        m["xv"] = xv
        m["cs"] = csv
        m["kmask"] = np.ascontiguousarray(kmv.reshape(cfg.nv // 128, 128).T)
        m["lmask"] = lm
        maps.append(m)
    return maps


def assemble(cfg, ys, inp):
    outs = []
    shapes = [np.asarray(inp["x_prompt"]).shape, np.asarray(inp["x_sample"]).shape]
    full = [np.zeros((L, D), np.float32) for L in cfg.seq_lens]
    for c in range(cfg.ncores):
        for s in range(len(cfg.seq_lens)):
            lo = cfg.lo[s]
            full[s][c * lo:(c + 1) * lo] = ys[c][cfg.ooff[s]:cfg.ooff[s] + lo]
    y_prompt = full[0].reshape(shapes[0])
    y_sample = np.stack(full[1:], axis=0).reshape(shapes[1])
    return y_prompt, y_sample


_CACHE = {}


def kernel(**inputs):
    cfg = Cfg()
    if "nc" not in _CACHE:
        _CACHE["nc"] = build_program(cfg)
    nc = _CACHE["nc"]
    maps = prep_inputs(cfg, inputs)
    res = run_bass_kernel_spmd(nc, maps, core_ids=list(range(cfg.ncores)))
    ys = [np.asarray(r["y"], np.float32) for r in res.results]
    return assemble(cfg, ys, inputs)
```
